# Optimizing a Trainium2 kernel written in Bass

```python
import math
import jax, jax.numpy as jnp
from jax import lax
import numpy as np

D_MODEL = 1024
BATCH = 2
SEQ = 8192
DEPTH = 1

PLE_DIM = 256
MIX_WIDTH = D_MODEL
POOL_WIDTH = MIX_WIDTH // 2
POOL_WINDOWS = (2, 4, 8, 16)
N_POOL_GROUPS = len(POOL_WINDOWS)
POOL_GROUP = POOL_WIDTH // N_POOL_GROUPS
HEAD_DIM = 64
ATTN_WIDTH = MIX_WIDTH - POOL_WIDTH
N_Q_HEADS = ATTN_WIDTH // HEAD_DIM
N_KV_HEADS = 2
GQA_GROUP = N_Q_HEADS // N_KV_HEADS
KV_WIDTH = N_KV_HEADS * HEAD_DIM
IN_WIDTH = POOL_WIDTH + ATTN_WIDTH + 2 * KV_WIDTH
WINDOW = 128
BLOCK = 128
D_FF = 4 * D_MODEL
LN_EPS = 1e-5
DEEPNORM_ALPHA = (2 * DEPTH) ** 0.25
DEEPNORM_BETA = (8 * DEPTH) ** -0.25
NEG_INF = -1e30

kernel_name = "hymba_pool_swa_deepnorm_layer"


def layer_norm(x, g, b):
    xf = x.astype(jnp.float32)
    mu = jnp.mean(xf, axis=-1, keepdims=True)
    xc = xf - mu
    var = jnp.mean(xc * xc, axis=-1, keepdims=True)
    y = xc * lax.rsqrt(var + LN_EPS) * g.astype(jnp.float32) + b.astype(jnp.float32)
    return y.astype(x.dtype)


def alibi_slopes(n_heads):
    h = jnp.arange(1, n_heads + 1, dtype=jnp.float32)
    return jnp.exp2(-8.0 * h / n_heads)


def pool_mixer(u, w_pool, pool_scale):
    B, S, _ = u.shape
    uf = u.astype(jnp.float32)
    cs = jnp.pad(jnp.cumsum(uf, axis=1), ((0, 0), (1, 0), (0, 0)))
    t = jnp.arange(S)
    outs = []
    for g, w in enumerate(POOL_WINDOWS):
        sl = slice(g * POOL_GROUP, (g + 1) * POOL_GROUP)
        c = cs[..., sl]
        lo = jnp.maximum(t + 1 - w, 0)
        win_sum = c[:, 1:] - c[:, lo]
        count = (t + 1 - lo).astype(jnp.float32)[None, :, None]
        outs.append(win_sum / count - uf[..., sl])
    d = jnp.stack(outs, axis=2).astype(u.dtype)
    y = jnp.einsum('bsgc,gcd->bsgd', d, w_pool).reshape(B, S, POOL_WIDTH)
    return y * pool_scale


def sliding_window_attention(q, k, v, sinks):
    B, S = q.shape[:2]
    nb = S // BLOCK
    qb = q.reshape(B, nb, BLOCK, N_KV_HEADS, GQA_GROUP, HEAD_DIM)

    def band(a):
        ab = a.reshape(B, nb, BLOCK, N_KV_HEADS, HEAD_DIM)
        prev = jnp.pad(ab, ((0, 0), (1, 0), (0, 0), (0, 0), (0, 0)))[:, :-1]
        return jnp.concatenate([prev, ab], axis=2)

    kb, vb = band(k), band(v)
    scale = 1.0 / math.sqrt(HEAD_DIM)
    scores = jnp.einsum('bnqkgd,bnjkd->bnkgqj', qb, kb).astype(jnp.float32) * scale

    qi = jnp.arange(BLOCK)
    kj = jnp.arange(2 * BLOCK)
    dist = (qi[:, None] + BLOCK - kj[None, :])
    in_band = (dist >= 0) & (dist < WINDOW)
    key_pos = jnp.arange(nb)[:, None] * BLOCK - BLOCK + kj[None, :]
    mask = in_band[None] & (key_pos >= 0)[:, None, :]

    slopes = alibi_slopes(N_Q_HEADS).reshape(N_KV_HEADS, GQA_GROUP)
    bias = -slopes[:, :, None, None] * dist.astype(jnp.float32)[None, None]
    scores = jnp.where(mask[None, :, None, None], scores + bias[None, None], NEG_INF)

    sink = sinks.astype(jnp.float32).reshape(1, 1, N_KV_HEADS, GQA_GROUP, 1, 1)
    m = jnp.maximum(jnp.max(scores, axis=-1, keepdims=True), sink)
    e = jnp.exp(scores - m)
    denom = jnp.sum(e, axis=-1, keepdims=True) + jnp.exp(sink - m)
    probs = (e / denom).astype(v.dtype)
    out = jnp.einsum('bnkgqj,bnjkd->bnqkgd', probs, vb)
    return out.reshape(B, S, ATTN_WIDTH)


def setup_inputs(seed: int = 0) -> dict:
    key = jax.random.key(seed)
    ks = jax.random.split(key, 16)
    f32 = jnp.float32

    def nrm(k, shape, std):
        return jax.random.normal(k, shape, f32) * std

    return {
        "x": nrm(ks[0], (BATCH, SEQ, D_MODEL), 1.0),
        "p": nrm(ks[1], (DEPTH, BATCH, SEQ, PLE_DIM), 1.0),
        "w_in": nrm(ks[2], (DEPTH, D_MODEL, IN_WIDTH), D_MODEL ** -0.5),
        "w_pool": nrm(ks[3], (DEPTH, N_POOL_GROUPS, POOL_GROUP, POOL_GROUP), POOL_GROUP ** -0.5),
        "pool_scale": 1.0 + nrm(ks[4], (DEPTH, POOL_WIDTH), 0.1),
        "attn_sinks": nrm(ks[5], (DEPTH, N_Q_HEADS), 0.5),
        "w_out": nrm(ks[6], (DEPTH, MIX_WIDTH, D_MODEL), MIX_WIDTH ** -0.5 * DEEPNORM_BETA),
        "ln1_g": 1.0 + nrm(ks[7], (DEPTH, D_MODEL), 0.05),
        "ln1_b": nrm(ks[8], (DEPTH, D_MODEL), 0.02),
        "w_ff1": nrm(ks[9], (DEPTH, D_MODEL, D_FF), D_MODEL ** -0.5),
        "w_ff2": nrm(ks[10], (DEPTH, D_FF, D_MODEL), D_FF ** -0.5 * DEEPNORM_BETA),
        "ln2_g": 1.0 + nrm(ks[11], (DEPTH, D_MODEL), 0.05),
        "ln2_b": nrm(ks[12], (DEPTH, D_MODEL), 0.02),
        "w_ple": nrm(ks[13], (DEPTH, PLE_DIM, D_MODEL), PLE_DIM ** -0.5),
        "w_ple_gate": nrm(ks[14], (DEPTH, D_MODEL, D_MODEL), D_MODEL ** -0.5),
        "b_ple_gate": nrm(ks[15], (DEPTH, D_MODEL), 0.02),
    }


def reference(x, p, w_in, w_pool, pool_scale, attn_sinks, w_out, ln1_g, ln1_b,
              w_ff1, w_ff2, ln2_g, ln2_b, w_ple, w_ple_gate, b_ple_gate):
    B, S, _ = x.shape
    h = x
    for i in range(DEPTH):
        z = h @ w_in[i]
        u_pool = z[..., :POOL_WIDTH]
        q = z[..., POOL_WIDTH:POOL_WIDTH + ATTN_WIDTH].reshape(B, S, N_Q_HEADS, HEAD_DIM)
        k = z[..., POOL_WIDTH + ATTN_WIDTH:POOL_WIDTH + ATTN_WIDTH + KV_WIDTH].reshape(B, S, N_KV_HEADS, HEAD_DIM)
        v = z[..., POOL_WIDTH + ATTN_WIDTH + KV_WIDTH:].reshape(B, S, N_KV_HEADS, HEAD_DIM)

        y_pool = pool_mixer(u_pool, w_pool[i], pool_scale[i])
        y_attn = sliding_window_attention(q, k, v, attn_sinks[i])
        mix = jnp.concatenate([y_pool, y_attn], axis=-1) @ w_out[i]
        h = layer_norm(DEEPNORM_ALPHA * h + mix, ln1_g[i], ln1_b[i])

        ff = jnp.square(jax.nn.relu(h @ w_ff1[i])) @ w_ff2[i]
        h = layer_norm(DEEPNORM_ALPHA * h + ff, ln2_g[i], ln2_b[i])

        gate = jax.nn.sigmoid(h @ w_ple_gate[i] + b_ple_gate[i])
        h = h + gate * (p[i] @ w_ple[i])
    return h
```

```python
import numpy as np
import concourse.bass as bass
import concourse.mybir as mybir
from contextlib import ExitStack
from concourse.bass_utils import run_bass_kernel_spmd

F32 = mybir.dt.float32
BF16 = mybir.dt.bfloat16
AF = mybir.ActivationFunctionType
ALU = mybir.AluOpType


class Buf:
    __slots__ = ("name", "ap", "last_write", "reads", "dsem", "is_psum")

    def __init__(self, name, ap=None, is_psum=False):
        self.name = name
        self.ap = ap
        self.is_psum = is_psum
        self.last_write = None
        self.reads = []
        self.dsem = None


class SemCounter:
    __slots__ = ("sem", "count", "name", "owner")

    def __init__(self, sem, name, owner=None):
        self.sem = sem
        self.count = 0
        self.name = name
        self.owner = owner


class Eng:
    def __init__(self, name, h, semc, inorder):
        self.name = name
        self.h = h
        self.semc = semc
        semc.owner = self
        self.waited = {}
        self.inorder = inorder
        self.n_wait = 0
        self.n_inst = 0
        self.prog = []


class FW:
    def __init__(self, nc, stack):
        self.nc = nc
        self.stack = stack
        self.engs = {}
        for name, h, inorder in (("pe", nc.tensor, True), ("act", nc.scalar, False),
                                 ("dve", nc.vector, False), ("pool", nc.gpsimd, False),
                                 ("sp", nc.sync, False)):
            sem = stack.enter_context(nc.semaphore("s_" + name))
            self.engs[name] = Eng(name, h, SemCounter(sem, "s_" + name), inorder)
        self.pe, self.act, self.dve, self.pool, self.sp = (
            self.engs[k] for k in ("pe", "act", "dve", "pool", "sp"))
        self.n_dsem = 0
        self.all_dsems = []
        self.collect = None
        self.waited_ev = {}
        self.evtime = {}
        self.dma_free = 0.0
        self.LAT = 1.0
        self.streams = []
        for e in self.engs.values():
            e.free = 0.0

    def sbuf(self, name, shape, dtype):
        t = self.stack.enter_context(self.nc.sbuf_tensor(name, list(shape), dtype))
        return Buf(name, t)

    def psum(self, name, shape, dtype):
        t = self.stack.enter_context(self.nc.psum_tensor(name, list(shape), dtype))
        return Buf(name, t, is_psum=True)

    def _dsem(self, buf):
        if buf.dsem is None:
            sem = self.stack.enter_context(self.nc.semaphore("d_" + buf.name))
            buf.dsem = SemCounter(sem, "d_" + buf.name)
            self.n_dsem += 1
            self.all_dsems.append(buf.dsem)
        return buf.dsem

    def _deps(self, eng, reads, writes):
        need = {}
        def add(ev, raw):
            if ev is None:
                return
            semc, val = ev
            if semc is eng.semc:
                if eng.inorder:
                    return
                if not raw:
                    return
            if need.get(semc, 0) < val:
                need[semc] = val
        for b in reads:
            add(b.last_write, True)
            if b.is_psum:
                for ev in b.reads:
                    if ev[0] is not eng.semc:
                        add(ev, False)
        for b in writes:
            add(b.last_write, False)
            for ev in b.reads:
                add(ev, False)
        return need

    def _ready_time(self, need):
        r = 0.0
        for semc, val in need.items():
            r = max(r, self.evtime.get((semc, val), 0.0) + self.LAT)
        return r

    def est_start(self, d):
        eng = d[1]
        return max(eng.free, self._ready_time(self._deps(eng, d[3], d[4])))

    def _wait_for(self, eng, reads, writes):
        need = self._deps(eng, reads, writes)
        ready = self._ready_time(need)
        for semc, val in need.items():
            if eng.waited.get(semc, 0) >= val:
                continue
            eng.prog.append(("w", semc, val))
            self.waited_ev.setdefault(semc, set()).add(val)
            eng.waited[semc] = val
            eng.n_wait += 1
        return ready

    def _record(self, ev, reads, writes):
        for b in reads:
            b.reads.append(ev)
        for b in writes:
            b.last_write = ev
            b.reads = []

    DEFCOST = {"pe": 0.215, "act": 0.65, "dve": 0.65, "pool": 1.2, "sp": 0.1}

    def op(self, eng, fn, reads=(), writes=(), signal=True, c=None, done=None):
        d = ("op", eng, fn, tuple(reads), tuple(writes), signal, c if c is not None else self.DEFCOST[eng.name], done)
        if self.collect is not None:
            self.collect.append(d)
        else:
            self.commit(d)

    def dma(self, eng, out_ap, in_ap, reads=(), writes=(), sem_buf=None, nbytes=1 << 20, on_commit=None, **kw):
        sb = sem_buf if sem_buf is not None else (writes[0] if writes else reads[0])
        d = ("dma", eng, (out_ap, in_ap, kw), tuple(reads), tuple(writes), sb, nbytes, on_commit)
        if self.collect is not None:
            self.collect.append(d)
        else:
            self.commit(d)

    def commit(self, d):
        eng, reads, writes = d[1], d[3], d[4]
        ready = self._wait_for(eng, reads, writes)
        start = max(eng.free, ready)
        eng.n_inst += 1
        if d[0] == "op":
            fn, signal, cost = d[2], d[5], d[6]
            fin = start + cost
            eng.free = fin
            if signal:
                eng.semc.count += 1
                eng.prog.append(("i", fn, eng.semc, eng.semc.count))
                ev = (eng.semc, eng.semc.count)
            else:
                eng.prog.append(("i", fn, None, 0))
                ev = (eng.semc, eng.semc.count + 1)
            self.evtime[ev] = max(self.evtime.get(ev, 0.0), fin)
            if d[7] is not None:
                d[7]()
        else:
            (out_ap, in_ap, kw), sb, nbytes, on_commit = d[2], d[5], d[6], d[7]
            semc = self._dsem(sb)
            semc.count += 16
            eng.prog.append(("i", (lambda h, o=out_ap, i=in_ap, k=kw: h.dma_start(out=o, in_=i, **k)),
                             semc, -16))
            issue_end = start + (1.0 if eng.name == "pool" else 0.1)
            eng.free = issue_end
            xs = max(issue_end, self.dma_free)
            self.dma_free = xs + nbytes / 300e3
            ev = (semc, semc.count)
            self.evtime[ev] = self.dma_free + 2.0
            if on_commit is not None:
                on_commit()
        self._record(ev, reads, writes)

    def schedule(self, gens):
        streams = [[g, None] for g in gens]
        self.streams = streams
        idle_rounds = 0
        while streams:
            progressed = False
            for st in list(streams):
                if st[1] is None:
                    self.collect = []
                    try:
                        next(st[0])
                        st[1] = self.collect
                    except StopIteration:
                        st[1] = self.collect
                        streams.remove(st)
                        if st[1]:
                            streams.append([iter(()), st[1]])
                    self.collect = None
            cands = [st for st in streams if st[1]]
            if not cands:
                for st in streams:
                    st[1] = None
                idle_rounds += 1
                assert idle_rounds < 10000, "scheduler: all streams blocked"
                continue
            idle_rounds = 0
            best = min(cands, key=lambda st: self.est_start(st[1][0]))
            for d in best[1]:
                self.commit(d)
            best[1] = None
            for st in streams:
                if st[1] is not None and not st[1]:
                    st[1] = None

    def spawn(self, g):
        self.streams.append([g, None])

    def final_wait(self, eng, bufs):
        for b in bufs:
            if b.dsem is not None:
                eng.prog.append(("w", b.dsem, b.dsem.count))

    def emit(self):
        eng_sems = {e.semc for e in self.engs.values()}
        rank = {}
        for semc in eng_sems:
            vals = sorted(self.waited_ev.get(semc, ()))
            rank[semc] = {v: i + 1 for i, v in enumerate(vals)}
        self.n_signals = {semc.name: len(rank[semc]) for semc in eng_sems}

        def replay(eng):
            def run(h):
                for it in eng.prog:
                    if it[0] == "w":
                        semc, val = it[1], it[2]
                        if semc in eng_sems:
                            h.wait_ge(semc.sem, rank[semc][val])
                        else:
                            h.wait_ge(semc.sem, val)
                    else:
                        inst = it[1](h)
                        semc, idx = it[2], it[3]
                        if semc is None:
                            continue
                        if idx < 0:
                            inst.then_inc(semc.sem, -idx)
                        elif idx in rank[semc]:
                            inst.then_inc(semc.sem, 1)
            return run
        with self.nc.Block() as block:
            block.tensor(replay(self.pe))
            block.scalar(replay(self.act))
            block.vector(replay(self.dve))
            block.gpsimd(replay(self.pool))
            block.sync(replay(self.sp))


D = 1024
TOK = 2048
NBLK = 16
ALPHA = 2.0 ** 0.25
EPS = 1e-5
NSLOT = 6


class _Stop(Exception):
    pass


def build_nc(stop=None):
    import os
    stop = stop or os.environ.get("KSTOP")
    nc = bass.Bass("TRN2", target_bir_lowering=False)

    def ck(name):
        if stop == name:
            raise _Stop()

    def din(name, shape):
        return nc.dram_tensor(name, list(shape), F32, kind="ExternalInput").ap()

    x = din("x", [TOK + 128, D])
    p = din("p", [TOK, 256])
    w_in = din("w_in", [D, 1280])
    w_pool = din("w_pool", [4, 128, 128])
    pscale_d = din("pool_scale", [128, 4])
    sinks_d = din("attn_sinks", [128, 8])
    w_out = din("w_out", [D, D])
    g1_d = din("ln1_g", [128, D]); b1_d = din("ln1_b", [128, D])
    w_ff1 = din("w_ff1", [D, 4096]); w_ff2 = din("w_ff2", [4096, D])
    g2_d = din("ln2_g", [128, D]); b2_d = din("ln2_b", [128, D])
    w_ple = din("w_ple", [256, D]); w_gate = din("w_ple_gate", [D, D])
    bg_d = din("b_ple_gate", [1, D])
    ident_d = din("c_ident", [128, 128])
    mask_d = din("c_mask", [128, 4, 512])
    maskf_d = din("c_mask_first", [128, 4, 512])
    apool_d = din("c_apool", [128, 8, 128])
    apoolf_d = din("c_apool_first", [128, 4, 128])
    out = nc.dram_tensor("out", [TOK, D], F32, kind="ExternalOutput").ap()

    win3 = w_in.rearrange("(k p) c -> p k c", p=128)
    wo3 = w_out.rearrange("(k p) c -> p k c", p=128)
    w13 = w_ff1.rearrange("(k p) c -> p k c", p=128)
    w23 = w_ff2.rearrange("(k p) c -> p k c", p=128)
    wg3 = w_gate.rearrange("(k p) c -> p k c", p=128)
    wp3 = w_ple.rearrange("(k p) c -> p k c", p=128)

    with ExitStack() as st:
        fw = FW(nc, st)
        PE, ACT, DVE, POOL, SP = fw.pe, fw.act, fw.dve, fw.pool, fw.sp

        ident = fw.sbuf("ident", [128, 128], BF16)
        Mk = fw.sbuf("Mk", [128, 4, 512], BF16)
        Mf = fw.sbuf("Mf", [128, 4, 512], BF16)
        Ap = fw.sbuf("Ap", [128, 8, 128], BF16)
        Apf = fw.sbuf("Apf", [128, 4, 128], BF16)
        Wp = fw.sbuf("Wp", [128, 4, 128], BF16)
        G1 = fw.sbuf("G1", [128, D], F32); B1 = fw.sbuf("B1", [128, D], F32)
        G2 = fw.sbuf("G2", [128, D], F32); B2 = fw.sbuf("B2", [128, D], F32)
        esink = fw.sbuf("esink", [128, 8], F32)
        pscale = fw.sbuf("pscale", [128, 4], F32)
        bghl = fw.sbuf("bghl", [1, 2 * D], BF16)
        ones1 = fw.sbuf("ones1", [1, 128], BF16)
        cm = fw.sbuf("cm", [128, 1], F32)

        fw.dma(POOL, ident.ap[:], ident_d, writes=[ident])
        def late_consts():
            fw.dma(POOL, Ap.ap[:], apool_d, writes=[Ap])
            fw.dma(POOL, Apf.ap[:], apoolf_d, writes=[Apf])
            fw.dma(POOL, Wp.ap[:], w_pool.rearrange("g c d -> c g d"), writes=[Wp])
            fw.dma(POOL, Mk.ap[:], mask_d, writes=[Mk])
            fw.dma(POOL, Mf.ap[:], maskf_d, writes=[Mf])
        def late_consts_sp():
            fw.dma(SP, G1.ap[:], g1_d, writes=[G1], nbytes=1 << 19); fw.dma(SP, B1.ap[:], b1_d, writes=[B1], nbytes=1 << 19)
            fw.dma(SP, G2.ap[:], g2_d, writes=[G2], nbytes=1 << 19); fw.dma(SP, B2.ap[:], b2_d, writes=[B2], nbytes=1 << 19)
        fw.dma(SP, esink.ap[:], sinks_d, writes=[esink])
        fw.dma(SP, pscale.ap[:], pscale_d, writes=[pscale])
        fw.op(ACT, lambda h: h.activation(esink.ap[:], esink.ap[:], AF.Exp), reads=[esink], writes=[esink])
        fw.op(DVE, lambda h: h.memset(cm.ap[:], -0.5), writes=[cm])
        fw.op(DVE, lambda h: h.memset(ones1.ap[:], 1.0), writes=[ones1])

        xb = [fw.sbuf("xb%d" % i, [128, D], BF16) for i in range(2)]
        xf = [fw.sbuf("xf%d" % i, [128, D], F32) for i in range(2)]
        xT = fw.sbuf("xT", [128, 8, 512], BF16)
        xTh = fw.sbuf("xTh", [128, 8, 128], BF16)
        Wkd = fw.sbuf("Wkd", [128, 8, 256], BF16)
        kT = [fw.sbuf("kT%d" % i, [128, 2, 512], BF16) for i in range(2)]
        Vv = [fw.sbuf("V%d" % i, [128, 4, 130], BF16) for i in range(2)]
        uu = [fw.sbuf("u%d" % i, [128, 4, 512], BF16) for i in range(2)]
        lnb = {}
        for nm in ("A0", "A1", "C0", "C1", "C2", "C3"):
            lnb[nm] = (fw.sbuf("st" + nm, [128, 12], F32), fw.sbuf("mv" + nm, [128, 2], F32),
                       fw.sbuf("ve" + nm, [128, 1], F32), fw.sbuf("rstd" + nm, [128, 1], F32))
        dens = [fw.sbuf("den%d" % j, [128, 8], F32) for j in range(2)]
        rdens = [fw.sbuf("rden%d" % j, [128, 8], F32) for j in range(2)]
        y1 = fw.sbuf("y1", [128, D], F32)
        h1 = [fw.sbuf("h1_%d" % i, [128, D], F32) for i in range(4)]
        hbA = fw.sbuf("hbA", [128, D], BF16)
        hbC = [fw.sbuf("hbC%d" % i, [128, D], BF16) for i in range(2)]
        rl = [fw.sbuf("rl%d" % i, [128, 512], F32) for i in range(2)]
        h2 = [fw.sbuf("h2%d" % i, [128, D], F32) for i in range(2)]
        h2T = [fw.sbuf("h2T%d" % i, [128, 8, 128], BF16) for i in range(2)]
        pb = fw.sbuf("pb", [128, 4, 256], BF16)
        pT = [fw.sbuf("pT%d" % i, [128, 2, 128], BF16) for i in range(2)]
        tt = [[fw.sbuf("tt%d_%d" % (j, i), [128, 512], F32) for i in range(2)] for j in range(2)]
        ovl = st.enter_context(nc.sbuf_tensor("ovl", [128, 32 * 512], BF16))
        aT = Buf("aT", ovl)
        o = 0
        def carve(name, n):
            nonlocal o
            b = Buf(name, ovl[:, o:o + n]); o += n
            return b
        yTo = o
        yTb = [carve("yT%d" % i, 8 * 128) for i in range(4)]
        dT = carve("dT", 4 * 512); qT = carve("qT", 4 * 512)
        EEs = [[carve("E%d_%d" % (j, i), 512) for i in range(4)] for j in range(2)]
        yatts = [carve("yatt%d" % j, 512) for j in range(2)]
        y1b = carve("y1b", 2048)
        hbAb = carve("hbAb", 1024)
        assert o <= 32 * 512, o
        stageA = yTb + [dT, qT, y1b, hbAb] + yatts + EEs[0] + EEs[1]
        aT3 = ovl[:, :].rearrange("p (c t) -> p c t", t=512)
        yT4 = ovl[:, yTo:yTo + 4096].rearrange("p (b c t) -> p b c t", b=4, c=8)
        dT3 = dT.ap.rearrange("p (c t) -> p c t", t=512)
        qT3 = qT.ap.rearrange("p (c t) -> p c t", t=512)

        def handoff(src, dst):
            for d in dst:
                for s in src:
                    if s.last_write is not None:
                        d.reads.append(s.last_write)
                    d.reads.extend(s.reads)

        for v in Vv:
            fw.op(DVE, lambda h, v=v: h.memset(v.ap[:], 1.0), writes=[v])

        fw.dma(SP, y1.ap[0:1, :], bg_d, writes=[y1])
        fw.op(DVE, lambda h: h.tensor_copy(bghl.ap[0:1, 0:D], y1.ap[0:1, :]), reads=[y1], writes=[bghl])
        fw.op(DVE, lambda h: h.tensor_copy(h2[0].ap[0:1, :], bghl.ap[0:1, 0:D]), reads=[bghl], writes=[h2[0]])
        fw.op(DVE, lambda h: h.tensor_tensor(h2[0].ap[0:1, :], y1.ap[0:1, :], h2[0].ap[0:1, :], op=ALU.subtract),
              reads=[y1, h2[0]], writes=[h2[0]])
        fw.op(DVE, lambda h: h.tensor_copy(bghl.ap[0:1, D:2 * D], h2[0].ap[0:1, :]), reads=[h2[0]], writes=[bghl])

        banks = [fw.psum("bk%d" % i, [128, 512], F32) for i in range(8)]
        bstate = {"i": 0, "held": set()}

        def nbg(n=1):
            while True:
                free = [(bstate["i"] + j) % 8 for j in range(8) if ((bstate["i"] + j) % 8) not in bstate["held"]]
                if len(free) >= n:
                    break
                yield
            got = free[:n]
            bstate["i"] = (got[-1] + 1) % 8
            for g_ in got:
                bstate["held"].add(g_)
            return [banks[g_] for g_ in got]

        def fb(bk):
            return lambda: bstate["held"].discard(banks.index(bk))

        slots = [fw.sbuf("ws%d" % i, [128, 4096], BF16) for i in range(NSLOT)]
        pieces = []
        for t in range(4):
            pieces += [(win3[:, :, 512:1024], 8, 512), (win3[:, :, 1024:1152], 8, 128),
                       (win3[:, :, 0:512], 8, 512), (win3[:, :, 1152:1280], 8, 128),
                       (wo3[:, :, 0:512], 8, 512), (wo3[:, :, 512:1024], 8, 512)]
            pieces += [(w13[:, :, fg * 512:(fg + 1) * 512], 8, 512) for fg in range(8)]
            pieces += [(w23[:, cg * 8:(cg + 1) * 8, hf * 512:(hf + 1) * 512], 8, 512)
                       for hf in range(2) for cg in range(4)]
            pieces += [(wg3[:, :, 0:512], 8, 512), (wg3[:, :, 512:1024], 8, 512), (wp3, 2, 1024)]
        ring = {"load": 0}
        free_slots = list(range(NSLOT))
        piece_slot = {}
        piece_ok = set()

        def pview(i, sl):
            src, k, c = pieces[i]
            return slots[sl].ap[:, 0:k * c].rearrange("p (k c) -> p k c", c=c)

        def try_load(maxn=None):
            n = 0
            while free_slots and ring["load"] < len(pieces) and (maxn is None or n < maxn):
                i = ring["load"]; ring["load"] += 1
                sl = free_slots.pop(0)
                piece_slot[i] = sl
                k, c = pieces[i][1], pieces[i][2]
                fw.dma(POOL, pview(i, sl), pieces[i][0], writes=[slots[sl]], nbytes=128 * k * c * 4,
                       on_commit=lambda i=i: piece_ok.add(i))
                n += 1

        def acquire(i):
            while i not in piece_ok:
                yield
            sl = piece_slot[i]
            return slots[sl], pview(i, sl)

        def release(i):
            free_slots.append(piece_slot[i])
            try_load()

        PT = 25
        cnt = {"ev": 0}

        def evac(dst_ap, src_ap, bk, writes, eng=None, c=0.65):
            if eng is None:
                eng = DVE if (cnt["ev"] % 4 == 3) else ACT
                cnt["ev"] += 1
            if eng is ACT:
                fw.op(ACT, lambda h: h.activation(dst_ap, src_ap, AF.Copy), reads=[bk], writes=writes, c=c, done=fb(bk))
            else:
                fw.op(DVE, lambda h: h.tensor_copy(dst_ap, src_ap), reads=[bk], writes=writes, c=c, done=fb(bk))

        def transpose_to(src_buf, src_ap_fn, n, dst_buf, dst_ap, eng=None):
            (bk,) = yield from nbg()
            pv = bk.ap[:].bitcast(BF16)
            for k in range(n):
                fw.op(PE, lambda h, k=k: h.transpose(pv[:, k * 128:(k + 1) * 128], src_ap_fn(k), ident.ap[:]),
                      reads=[src_buf, ident], writes=[bk], signal=(k == n - 1), c=0.07)
            yield
            evac(dst_ap, pv[:, 0:n * 128].rearrange("p (k t) -> p k t", t=128), bk, [dst_buf], eng, c=0.2 + n * 0.07)
            yield

        def load_xb(gbl):
            s = xb[(gbl + 1) % 2]
            fw.dma(POOL, s.ap[:], x[(gbl + 1) * 128:(gbl + 2) * 128, :], writes=[s], nbytes=1 << 19)
            return s

        def load_xf(gbl):
            s = xf[gbl % 2]
            fw.dma(SP, s.ap[:], x[(gbl + 1) * 128:(gbl + 2) * 128, :], writes=[s], nbytes=1 << 19)
            return s

        def uv_block(xT_ap_fn, xT_buf, wu_b, wu, wv_b, wv, par, blk):
            bu, bv = yield from nbg(2)
            for k in range(8):
                fw.op(PE, lambda h, k=k: h.matmul(bu.ap[:], xT_ap_fn(k), wu[:, k, :], start=(k == 0), stop=(k == 7)),
                      reads=[xT_buf, wu_b], writes=[bu], signal=(k == 7))
            for k in range(8):
                fw.op(PE, lambda h, k=k: h.matmul(bv.ap[:, 0:128], xT_ap_fn(k), wv[:, k, :], start=(k == 0), stop=(k == 7)),
                      reads=[xT_buf, wv_b], writes=[bv], signal=(k == 7), c=0.07)
            yield
            evac(uu[par].ap[:, blk, :], bu.ap[:], bu, [uu[par]])
            evac(Vv[par].ap[:, blk, :].rearrange("p (h d) -> p h d", d=65)[:, :, 0:64],
                 bv.ap[:, 0:128].rearrange("p (h d) -> p h d", d=64), bv, [Vv[par]], c=0.3)
            yield

        def layernorm(nm, src, work, G, B, dst, hbuf):
            stb, mvb, veb, rsb = lnb[nm]
            (sb, sap), (wb, wap), (db, dap) = src, work, dst
            for i in range(2):
                fw.op(DVE, lambda h, i=i: h.bn_stats(stb.ap[:, i * 6:(i + 1) * 6], sap[:, i * 512:(i + 1) * 512]),
                      reads=[sb], writes=[stb], c=0.65)
            fw.op(DVE, lambda h: h.bn_aggr(mvb.ap[:], stb.ap[:]), reads=[stb], writes=[mvb], c=0.2)
            yield
            fw.op(POOL, lambda h: h.tensor_scalar(veb.ap[:], mvb.ap[:, 1:2], EPS, None, op0=ALU.add), reads=[mvb], writes=[veb], c=0.2)
            fw.op(POOL, lambda h: h.tensor_tensor(rsb.ap[:], veb.ap[:], cm.ap[:], op=ALU.pow), reads=[veb, cm], writes=[rsb], c=0.5)
            yield
            fw.op(DVE, lambda h: h.tensor_scalar(wap, sap, mvb.ap[:, 0:1], rsb.ap[:, 0:1],
                                                 op0=ALU.subtract, op1=ALU.mult), reads=[sb, mvb, rsb], writes=[wb], c=0.8)
            fw.op(DVE, lambda h: h.tensor_tensor(wap, wap, G.ap[:], op=ALU.mult), reads=[wb, G], writes=[wb], c=1.2)
            yield
            fw.op(DVE, lambda h: h.tensor_tensor(hbuf[1], wap, B.ap[:], op=ALU.add), reads=[wb, B], writes=[hbuf[0]], c=1.2)
            yield
            fw.op(POOL, lambda h: h.tensor_tensor(dap, wap, B.ap[:], op=ALU.add), reads=[wb, B], writes=[db], c=2.4)
            yield

        prog = {"c_read": -1, "attn": -1, "wln": -1, "B": -1}
        wost = {}
        attn_done = set()
        wln_cnt = {}
        st_ = {"xb_next": None}

        def gen_Apre(t):
            par = t % 2
            p0 = t * PT
            if t == 0:
                s = st_["xb_next"]
                st_["xb_next"] = load_xb(0)
                try_load(2)
                yield from transpose_to(s, lambda k, s=s: s.ap[:, k * 128:(k + 1) * 128], 8, xTh, xTh.ap[:])
            for b in range(4):
                gbl = t * 4 + b
                s = st_["xb_next"]
                if gbl + 1 < NBLK:
                    st_["xb_next"] = load_xb(gbl + 1)
                if t == 0 and b == 2:
                    try_load()
                    late_consts()
                    late_consts_sp()
                yield from transpose_to(s, lambda k, s=s: s.ap[:, k * 128:(k + 1) * 128], 8, xT, xT.ap[:, :, b * 128:(b + 1) * 128])
            wk_b, wk = yield from acquire(p0 + 1)
            for kvh in range(2):
                for r in range(2):
                    fw.op(DVE, lambda h, kvh=kvh, r=r: h.tensor_copy(Wkd.ap[:, :, kvh * 128 + r * 64:kvh * 128 + r * 64 + 64],
                                                                    wk[:, :, kvh * 64:(kvh + 1) * 64]),
                          reads=[wk_b], writes=[Wkd], c=0.4)
            yield
            release(p0 + 1)
            for kvh in range(2):
                (bk,) = yield from nbg()
                for k in range(8):
                    fw.op(PE, lambda h, k=k, kvh=kvh, bk=bk: h.matmul(bk.ap[:], Wkd.ap[:, k, kvh * 128:(kvh + 1) * 128], xT.ap[:, k, :],
                                                                      start=(k == 0), stop=(k == 7)),
                          reads=[Wkd, xT], writes=[bk], signal=(k == 7))
                yield
                evac(kT[par].ap[:, kvh, :], bk.ap[:], bk, [kT[par]])
                yield
            if t == 0:
                (bk,) = yield from nbg()
                for kvh in range(2):
                    for k in range(8):
                        fw.op(PE, lambda h, k=k, kvh=kvh, bk=bk: h.matmul(bk.ap[:, kvh * 128:(kvh + 1) * 128],
                                                                          Wkd.ap[:, k, kvh * 128:(kvh + 1) * 128], xTh.ap[:, k, :],
                                                                          start=(k == 0), stop=(k == 7)),
                              reads=[Wkd, xTh], writes=[bk], signal=(k == 7), c=0.07)
                yield
                evac(kT[1].ap[:, :, 384:512], bk.ap[:, 0:256].rearrange("p (h t) -> p h t", t=128), bk, [kT[1]], c=0.4)
                yield
            wu_b, wu = yield from acquire(p0 + 2)
            wv_b, wv = yield from acquire(p0 + 3)
            if t == 0:
                yield from uv_block(lambda k: xTh.ap[:, k, :], xTh, wu_b, wu, wv_b, wv, 1, 3)
            for b in range(4):
                yield from uv_block(lambda k, b=b: xT.ap[:, k, b * 128:(b + 1) * 128], xT, wu_b, wu, wv_b, wv, par, b)
            release(p0 + 2); release(p0 + 3)
            for b in range(4):
                first = (t == 0 and b == 0)
                if b == 0:
                    up_b, up = uu[1 - par], uu[1 - par].ap[:, 3, :]
                else:
                    up_b, up = uu[par], uu[par].ap[:, b - 1, :]
                uc = uu[par].ap[:, b, :]
                (bk,) = yield from nbg()
                for g in range(4):
                    acur_b, acur = (Apf, Apf.ap[:, g, :]) if first else (Ap, Ap.ap[:, 4 + g, :])
                    fw.op(PE, lambda h, g=g, bk=bk, up=up: h.matmul(bk.ap[:, g * 128:(g + 1) * 128], up[:, g * 128:(g + 1) * 128],
                                                                    Ap.ap[:, g, :], start=True, stop=False),
                          reads=[up_b, Ap], writes=[bk], signal=False, c=0.07)
                    fw.op(PE, lambda h, g=g, bk=bk, uc=uc, acur=acur: h.matmul(bk.ap[:, g * 128:(g + 1) * 128], uc[:, g * 128:(g + 1) * 128],
                                                                                 acur, start=False, stop=True),
                          reads=[uu[par], acur_b], writes=[bk], signal=(g == 3), c=0.07)
                yield
                evac(dT3[:, :, b * 128:(b + 1) * 128], bk.ap[:].rearrange("p (g t) -> p g t", t=128), bk, [dT])
                yield
            for g in range(4):
                (bk,) = yield from nbg()
                fw.op(PE, lambda h, g=g, bk=bk: h.matmul(bk.ap[:], Wp.ap[:, g, :], dT3[:, g, :], start=True, stop=True),
                      reads=[Wp, dT], writes=[bk])
                yield
                fw.op(ACT, lambda h, g=g, bk=bk: h.activation(yT4[:, :, g, :], bk.ap[:].rearrange("p (b t) -> p b t", t=128),
                                                              AF.Copy, scale=pscale.ap[:, g:g + 1]),
                      reads=[bk, pscale], writes=yTb, done=fb(bk))
                yield
            fw.op(POOL, lambda h: h.memset(qT3[64:128, :, :], 0.0), writes=[qT], c=1.0)
            fw.op(POOL, lambda h: h.memset(dT3[0:64, :, :], 0.0), writes=[dT], c=1.0)
            yield
            wq_b, wq = yield from acquire(p0 + 0)
            for qc in range(4):
                (bk,) = yield from nbg()
                for k in range(8):
                    fw.op(PE, lambda h, k=k, qc=qc, bk=bk: h.matmul(bk.ap[:], wq[:, k, qc * 128:(qc + 1) * 128], xT.ap[:, k, :],
                                                                    start=(k == 0), stop=(k == 7)),
                          reads=[wq_b, xT], writes=[bk], signal=(k == 7))
                yield
                fw.op(ACT, lambda h, qc=qc, bk=bk: h.activation(qT3[0:64, qc, :], bk.ap[0:64, :], AF.Copy),
                      reads=[bk], writes=[qT], c=0.65)
                fw.op(ACT, lambda h, qc=qc, bk=bk: h.activation(dT3[64:128, qc, :], bk.ap[64:128, :], AF.Copy),
                      reads=[bk], writes=[dT], c=0.65, done=fb(bk))
                yield
            release(p0 + 0)
            wo0_b, wo0 = yield from acquire(p0 + 4)
            wo1_b, wo1 = yield from acquire(p0 + 5)
            wost[t] = [(wo0_b, wo0), (wo1_b, wo1)]

        def gen_attn(t, s2):
            par = t % 2
            EE = EEs[s2]; yatt = yatts[s2]; den = dens[s2]; rden = rdens[s2]
            for b in (s2, s2 + 2):
                gbl = t * 4 + b
                first = (gbl == 0)
                if b == 0:
                    kp_b, kp = kT[1 - par], kT[1 - par].ap[:, :, 384:512]
                    vp_b, vp = Vv[1 - par], Vv[1 - par].ap[:, 3, :]
                else:
                    kp_b, kp = kT[par], kT[par].ap[:, :, (b - 1) * 128:b * 128]
                    vp_b, vp = Vv[par], Vv[par].ap[:, b - 1, :]
                kc = kT[par].ap[:, :, b * 128:(b + 1) * 128]
                vc = Vv[par].ap[:, b, :]
                for kvh in range(2):
                    for half in range(2):
                        (bk,) = yield from nbg()
                        for kb in range(2):
                            kk_b, kk = (kp_b, kp) if kb == 0 else (kT[par], kc)
                            for gp in range(2):
                                qc = 2 * kvh + gp
                                col = (kb * 2 + gp) * 128
                                qz_b, qz3 = (qT, qT3) if half == 0 else (dT, dT3)
                                fw.op(PE, lambda h, bk=bk, col=col, kk=kk, kvh=kvh, qc=qc, qz3=qz3, b=b:
                                      h.matmul(bk.ap[:, col:col + 128], kk[:, kvh, :],
                                               qz3[:, qc, b * 128:(b + 1) * 128], start=True, stop=True),
                                      reads=[kk_b, qz_b], writes=[bk], signal=(kb == 1 and gp == 1), c=0.07)
                        yield
                        E = EE[kvh * 2 + half]
                        fw.op(ACT, lambda h, bk=bk, E=E: h.activation(E.ap, bk.ap[:], AF.Exp, scale=0.125), reads=[bk], writes=[E],
                              c=0.5, done=fb(bk))
                        yield
                        m_b = Mf if first else Mk
                        m = m_b.ap[:, kvh * 2 + half, :]
                        fw.op(DVE, lambda h, E=E, m=m: h.tensor_tensor(E.ap, E.ap, m, op=ALU.mult), reads=[E, m_b], writes=[E], c=0.43)
                        yield
                bos = yield from nbg(2)
                for kvh in range(2):
                    bo = bos[kvh]
                    for g in range(4):
                        gp, half = g // 2, g % 2
                        E = EE[kvh * 2 + half]
                        for kb in range(2):
                            col = (kb * 2 + gp) * 128
                            vv_b, vv = (vp_b, vp) if kb == 0 else (Vv[par], vc)
                            fw.op(PE, lambda h, bo=bo, g=g, E=E, vv=vv, kvh=kvh, kb=kb, col=col:
                                  h.matmul(bo.ap[:, g * 65:(g + 1) * 65], E.ap[:, col:col + 128],
                                           vv[:, kvh * 65:(kvh + 1) * 65], start=(kb == 0), stop=(kb == 1)),
                                  reads=[E, vv_b], writes=[bo], signal=(g == 3 and kb == 1), c=0.07)
                    yield
                for kvh in range(2):
                    bo = bos[kvh]
                    bo3 = bo.ap[:, 0:260].rearrange("p (h d) -> p h d", d=65)
                    fw.op(DVE, lambda h, bo3=bo3, kvh=kvh: h.tensor_tensor(den.ap[:, kvh * 4:(kvh + 1) * 4], bo3[:, :, 64],
                                                                           esink.ap[:, kvh * 4:(kvh + 1) * 4], op=ALU.add),
                          reads=[bo, esink], writes=[den], c=0.15)
                    fw.op(DVE, lambda h, kvh=kvh: h.reciprocal(rden.ap[:, kvh * 4:(kvh + 1) * 4], den.ap[:, kvh * 4:(kvh + 1) * 4]),
                          reads=[den], writes=[rden], c=0.2)
                    yield
                    for g in range(4):
                        hh = 4 * kvh + g
                        dn = fb(bo) if g == 3 else None
                        if False:
                            fw.op(DVE, lambda h, bo3=bo3, g=g, hh=hh: h.tensor_scalar(yatt.ap[:, hh * 64:(hh + 1) * 64], bo3[:, g, 0:64],
                                                                                      rden.ap[:, hh:hh + 1], None, op0=ALU.mult),
                                  reads=[bo, rden], writes=[yatt], c=0.22, done=dn)
                        else:
                            fw.op(ACT, lambda h, bo3=bo3, g=g, hh=hh: h.activation(yatt.ap[:, hh * 64:(hh + 1) * 64], bo3[:, g, 0:64],
                                                                                   AF.Copy, scale=rden.ap[:, hh:hh + 1]),
                                  reads=[bo, rden], writes=[yatt], c=0.32, done=dn)
                    yield
                yield from transpose_to(yatt, lambda k: yatt.ap[:, k * 128:(k + 1) * 128], 4, yTb[b], yT4[:, b, 4:8, :])
                attn_done.add(gbl)

        def gen_wln(t, s2):
            wo = wost[t]
            p0 = t * PT
            if s2 == 0:
                y1_b, y1_ap, hb_b, hb_ap = y1, y1.ap[:], hbA, hbA.ap[:]
            else:
                y1_b, y1_ap, hb_b, hb_ap = y1b, y1b.ap.bitcast(F32), hbAb, hbAb.ap
            xfs = load_xf(t * 4 + s2)
            for b in (s2, s2 + 2):
                gbl = t * 4 + b
                while gbl not in attn_done:
                    yield
                for hf in range(2):
                    (bk,) = yield from nbg()
                    wb_, wv_ = wo[hf]
                    for c in range(8):
                        fw.op(PE, lambda h, bk=bk, c=c, wv_=wv_, b=b: h.matmul(bk.ap[:], yT4[:, b, c, :], wv_[:, c, :],
                                                                              start=(c == 0), stop=(c == 7)),
                              reads=[yTb[b], wb_], writes=[bk], signal=(c == 7))
                    yield
                    fw.op(DVE, lambda h, bk=bk, hf=hf, xfs=xfs: h.scalar_tensor_tensor(y1_ap[:, hf * 512:(hf + 1) * 512],
                                                                                       xfs.ap[:, hf * 512:(hf + 1) * 512], ALPHA, bk.ap[:],
                                                                                       op0=ALU.mult, op1=ALU.add),
                          reads=[xfs, bk], writes=[y1_b], c=0.7, done=fb(bk))
                    yield
                if b == s2:
                    xfs = load_xf(gbl + 2)
                while gbl >= 4 and (gbl - 4) not in cst["read"]:
                    yield
                yield from layernorm("A%d" % s2, (y1_b, y1_ap), (y1_b, y1_ap), G1, B1, (h1[b], h1[b].ap[:]), (hb_b, hb_ap))
                yield from transpose_to(hb_b, lambda k: hb_ap[:, k * 128:(k + 1) * 128], 8, xT, xT.ap[:, :, b * 128:(b + 1) * 128])
            wln_cnt[t] = wln_cnt.get(t, 0) + 1
            if wln_cnt[t] == 2:
                release(p0 + 4); release(p0 + 5)
                yield
                prog["wln"] = t

        def gen_B(t):
            p0 = t * PT
            handoff(stageA, [aT])
            for fg in range(8):
                w1_b, w1 = yield from acquire(p0 + 6 + fg)
                for fc in range(4):
                    c = fg * 4 + fc
                    (bk,) = yield from nbg()
                    for k in range(8):
                        fw.op(PE, lambda h, bk=bk, k=k, fc=fc, w1=w1: h.matmul(bk.ap[:], w1[:, k, fc * 128:(fc + 1) * 128], xT.ap[:, k, :],
                                                                              start=(k == 0), stop=(k == 7)),
                              reads=[w1_b, xT], writes=[bk], signal=(k == 7))
                    yield
                    r = rl[c % 2]
                    fw.op(ACT, lambda h, bk=bk, r=r: h.activation(r.ap[:], bk.ap[:], AF.Relu), reads=[bk], writes=[r], c=0.6, done=fb(bk))
                    yield
                    fw.op(DVE, lambda h, r=r, c=c: h.tensor_tensor(aT3[:, c, :], r.ap[:], r.ap[:], op=ALU.mult), reads=[r], writes=[aT], c=0.62)
                    yield
                release(p0 + 6 + fg)
            for hf in range(2):
                acc = yield from nbg(4)
                for cg in range(4):
                    w2_b, w2 = yield from acquire(p0 + 14 + hf * 4 + cg)
                    for j in range(8):
                        c = cg * 8 + j
                        for b in range(4):
                            fw.op(PE, lambda h, c=c, j=j, b=b, w2=w2, acc=acc: h.matmul(acc[b].ap[:], aT3[:, c, b * 128:(b + 1) * 128], w2[:, j, :],
                                                                                        start=(c == 0), stop=(c == 31)),
                                  reads=[aT, w2_b], writes=[acc[b]], signal=(c == 31 or (j == 7 and b == 3)))
                        if j % 2 == 1:
                            yield
                    release(p0 + 14 + hf * 4 + cg)
                for b in range(4):
                    fw.op(DVE, lambda h, b=b, hf=hf, acc=acc: h.scalar_tensor_tensor(h1[b].ap[:, hf * 512:(hf + 1) * 512],
                                                                                     h1[b].ap[:, hf * 512:(hf + 1) * 512], ALPHA, acc[b].ap[:],
                                                                                     op0=ALU.mult, op1=ALU.add),
                          reads=[h1[b], acc[b]], writes=[h1[b]], c=0.7, done=fb(acc[b]))
                    yield
            handoff([aT], stageA)
            prog["B"] = t

        cst = {"done": [-1, -1, -1, -1], "read": set(), "pb": -1, "rel": {}}
        cbufs = {cp_: dict(hs=h2[cp_], hbc=hbC[cp_], hT=h2T[cp_], pT=pT[cp_], tt=tt[cp_]) for cp_ in range(2)}
        oo = 0
        for cp_ in (2, 3):
            def cv(name, n, f32=False, r3=None):
                nonlocal oo
                apv = ovl[:, oo:oo + n]
                oo += n
                if f32:
                    apv = apv.bitcast(F32)
                if r3:
                    apv = apv.rearrange("p (k t) -> p k t", t=r3)
                return Buf(name + str(cp_), apv)
            cbufs[cp_] = dict(hs=cv("xh2_", 2048, f32=True), hbc=cv("xhbC_", 1024), hT=cv("xh2T_", 1024, r3=128),
                              pT=cv("xpT_", 256, r3=128), tt=[cv("xtt0_", 1024, f32=True), cv("xtt1_", 1024, f32=True)])

        def gen_C(t, cp, blocks, nstreams):
            p0 = t * PT
            cb = cbufs[cp]
            wg0_b, wg0 = yield from acquire(p0 + 22)
            wg1_b, wg1 = yield from acquire(p0 + 23)
            wpl_b, wpl = yield from acquire(p0 + 24)
            wgs = [(wg0_b, wg0), (wg1_b, wg1)]
            if cp == 0:
                while cst["done"][1] < t - 1:
                    yield
                fw.dma(POOL, pb.ap[:], p[t * 512:(t + 1) * 512, :].rearrange("(b q) c -> q b c", q=128), writes=[pb], nbytes=1 << 19)
                yield
                cst["pb"] = t
            else:
                while cst["pb"] < t:
                    yield
            for b in blocks:
                gbl = t * 4 + b
                hs = cb["hs"]
                hbc = cb["hbc"]
                yield from layernorm("C%d" % cp, (h1[b], h1[b].ap[:]), (hs, hs.ap[:]), G2, B2, (hs, hs.ap[:]), (hbc, hbc.ap[:]))
                cst["read"].add(gbl)
                hT = cb["hT"]
                yield from transpose_to(hbc, lambda k, hbc=hbc: hbc.ap[:, k * 128:(k + 1) * 128], 8, hT, hT.ap[:])
                pTs = cb["pT"]
                yield from transpose_to(pb, lambda k, b=b: pb.ap[:, b, k * 128:(k + 1) * 128], 2, pTs, pTs.ap[:])
                for hf in range(2):
                    bg_, bp_ = yield from nbg(2)
                    wb_, wv_ = wgs[hf]
                    for k in range(8):
                        fw.op(PE, lambda h, bg_=bg_, k=k, wv_=wv_, hT=hT: h.matmul(bg_.ap[:], hT.ap[:, k, :], wv_[:, k, :], start=(k == 0), stop=False),
                              reads=[hT, wb_], writes=[bg_], signal=False)
                    for r in range(2):
                        fw.op(PE, lambda h, bg_=bg_, r=r, hf=hf: h.matmul(bg_.ap[:], ones1.ap[0:1, :],
                                                                          bghl.ap[0:1, r * D + hf * 512:r * D + (hf + 1) * 512],
                                                                          start=False, stop=(r == 1)),
                              reads=[ones1, bghl], writes=[bg_], signal=(r == 1))
                    for k in range(2):
                        fw.op(PE, lambda h, bp_=bp_, k=k, hf=hf, pTs=pTs: h.matmul(bp_.ap[:], pTs.ap[:, k, :], wpl[:, k, hf * 512:(hf + 1) * 512],
                                                                                   start=(k == 0), stop=(k == 1)),
                              reads=[pTs, wpl_b], writes=[bp_], signal=(k == 1))
                    yield
                    tts = cb["tt"][hf]
                    fw.op(ACT, lambda h, bg_=bg_, tts=tts: h.activation(tts.ap[:], bg_.ap[:], AF.Tanh, scale=0.5), reads=[bg_], writes=[tts],
                          c=0.6, done=fb(bg_))
                    yield
                    fw.op(DVE, lambda h, tts=tts, bp_=bp_: h.scalar_tensor_tensor(tts.ap[:], tts.ap[:], 1.0, bp_.ap[:], op0=ALU.add, op1=ALU.mult),
                          reads=[tts, bp_], writes=[tts], c=0.7, done=fb(bp_))
                    fw.op(DVE, lambda h, tts=tts, hs=hs, hf=hf: h.scalar_tensor_tensor(hs.ap[:, hf * 512:(hf + 1) * 512], tts.ap[:], 0.5,
                                                                                       hs.ap[:, hf * 512:(hf + 1) * 512], op0=ALU.mult, op1=ALU.add),
                          reads=[tts, hs], writes=[hs], c=0.7)
                    yield
                fw.dma(SP, out[gbl * 128:(gbl + 1) * 128, :], hs.ap[:], reads=[hs], sem_buf=hs, nbytes=1 << 19)
                yield
            cst["done"][cp] = t
            cst["rel"][t] = cst["rel"].get(t, 0) + 1
            if cst["rel"][t] == nstreams:
                release(p0 + 22); release(p0 + 23); release(p0 + 24)
                yield

        def main_line():
            for t in range(4):
                yield from gen_Apre(t)
                fw.spawn(gen_attn(t, 0))
                fw.spawn(gen_attn(t, 1))
                fw.spawn(gen_wln(t, 0))
                fw.spawn(gen_wln(t, 1))
                while prog["wln"] < t:
                    yield
                yield from gen_B(t)

        def c_line(cp):
            for t in range(4):
                while prog["B"] < t:
                    yield
                if t < 3:
                    yield from gen_C(t, cp, (cp, cp + 2), 2)
                else:
                    yield from gen_C(t, cp, (cp,), 4)

        def c_line_x(cp):
            while prog["B"] < 3:
                yield
            handoff([aT], [v for v in cbufs[cp].values() if isinstance(v, Buf)] + cbufs[cp]["tt"])
            yield from gen_C(3, cp, (cp,), 4)

        st_["xb_next"] = load_xb(-1)
        fw.schedule([main_line(), c_line(0), c_line(1), c_line_x(2), c_line_x(3)])
        fw.final_wait(SP, h2 + [cbufs[2]["hs"], cbufs[3]["hs"]])
        fw.emit()
        build_nc.stats = {e.name: (e.n_inst, e.n_wait, round(e.free, 1)) for e in fw.engs.values()}
    return nc


def _constants():
    j = np.arange(128)[:, None]
    q = np.arange(128)[None, :]
    mask = np.zeros((128, 2, 2, 2, 2, 128), np.float32)
    for kvh in range(2):
        for half in range(2):
            for gp in range(2):
                slope = 2.0 ** (-(4 * kvh + 2 * gp + half + 1))
                dprev = (q + 128 - j).astype(np.float32)
                dcur = (q - j).astype(np.float32)
                mask[:, kvh, half, 0, gp, :] = np.where(j > q, np.exp(-slope * dprev), 0.0)
                mask[:, kvh, half, 1, gp, :] = np.where(j <= q, np.exp(-slope * dcur), 0.0)
    apool = np.zeros((128, 2, 4, 128), np.float32)
    apool_first = np.zeros((128, 4, 128), np.float32)
    si = np.arange(128)[:, None]
    ti = np.arange(128)[None, :]
    for g, w in enumerate((2, 4, 8, 16)):
        apool[:, 0, g, :] = np.where(128 + ti - si < w, 1.0 / w, 0.0)
        cur = np.where((si <= ti) & (ti - si < w), 1.0 / w, 0.0)
        apool[:, 1, g, :] = cur - (si == ti)
        cnt = np.minimum(ti + 1, w).astype(np.float32)
        apool_first[:, g, :] = np.where((si <= ti) & (ti - si < w), 1.0 / cnt, 0.0) - (si == ti)
    return mask, apool, apool_first


def _make_in_maps(x, p, w_in, w_pool, pool_scale, attn_sinks, w_out, ln1_g, ln1_b,
                  w_ff1, w_ff2, ln2_g, ln2_b, w_ple, w_ple_gate, b_ple_gate):
    f = lambda a: np.ascontiguousarray(np.asarray(a, dtype=np.float32))
    x = f(x); p = f(p)
    mask, apool, apool_first = _constants()
    rep = lambda v: np.ascontiguousarray(np.broadcast_to(f(v).reshape(1, -1), (128, f(v).size)))
    shared = {
        "w_in": f(w_in)[0], "w_pool": f(w_pool)[0],
        "pool_scale": np.ascontiguousarray(f(pool_scale)[0].reshape(4, 128).T),
        "attn_sinks": rep(attn_sinks[0]), "w_out": f(w_out)[0],
        "ln1_g": rep(ln1_g[0]), "ln1_b": rep(ln1_b[0]), "w_ff1": f(w_ff1)[0], "w_ff2": f(w_ff2)[0],
        "ln2_g": rep(ln2_g[0]), "ln2_b": rep(ln2_b[0]), "w_ple": f(w_ple)[0], "w_ple_gate": f(w_ple_gate)[0],
        "b_ple_gate": f(b_ple_gate)[0].reshape(1, D),
        "c_ident": np.eye(128, dtype=np.float32),
        "c_mask": np.ascontiguousarray(mask.reshape(128, 4, 512)),
        "c_apool": np.ascontiguousarray(apool.reshape(128, 8, 128)),
    }
    in_maps = []
    for c in range(8):
        bi, j = c // 4, c % 4
        xs = np.zeros((TOK + 128, D), np.float32)
        lo = j * TOK
        xs[128:] = x[bi, lo:lo + TOK]
        if j > 0:
            xs[:128] = x[bi, lo - 128:lo]
        m = dict(shared)
        m["x"] = xs
        m["p"] = np.ascontiguousarray(p[0, bi, lo:lo + TOK])
        mf = mask.copy()
        if j == 0:
            mf[:, :, :, 0] = 0.0
            m["c_apool_first"] = np.ascontiguousarray(apool_first)
        else:
            m["c_apool_first"] = np.ascontiguousarray(apool[:, 1])
        m["c_mask_first"] = np.ascontiguousarray(mf.reshape(128, 4, 512))
        in_maps.append(m)
    return in_maps


def kernel(**inputs):
    in_maps = _make_in_maps(**inputs)
    nc = build_nc()
    res = run_bass_kernel_spmd(nc, in_maps, core_ids=list(range(8)))
    outp = np.zeros((2, 8192, D), np.float32)
    for c in range(8):
        bi, j = c // 4, c % 4
        outp[bi, j * TOK:(j + 1) * TOK] = res.results[c]["out"]
    return outp
```

```python
import numpy as np
import concourse.bass as bass
import concourse.mybir as mybir
from contextlib import ExitStack
from concourse.bass_utils import run_bass_kernel_spmd

F32 = mybir.dt.float32
BF16 = mybir.dt.bfloat16
AF = mybir.ActivationFunctionType
ALU = mybir.AluOpType


class Buf:
    __slots__ = ("name", "ap", "last_write", "reads", "dsem", "is_psum")

    def __init__(self, name, ap=None, is_psum=False):
        self.name = name
        self.ap = ap
        self.is_psum = is_psum
        self.last_write = None
        self.reads = []
        self.dsem = None


class SemCounter:
    __slots__ = ("sem", "count", "name", "owner")

    def __init__(self, sem, name, owner=None):
        self.sem = sem
        self.count = 0
        self.name = name
        self.owner = owner


class Eng:
    def __init__(self, name, h, semc, inorder):
        self.name = name
        self.h = h
        self.semc = semc
        semc.owner = self
        self.waited = {}
        self.inorder = inorder
        self.n_wait = 0
        self.n_inst = 0
        self.prog = []


class FW:
    def __init__(self, nc, stack):
        self.nc = nc
        self.stack = stack
        self.engs = {}
        for name, h, inorder in (("pe", nc.tensor, True), ("act", nc.scalar, False),
                                 ("dve", nc.vector, False), ("pool", nc.gpsimd, False),
                                 ("sp", nc.sync, False)):
            sem = stack.enter_context(nc.semaphore("s_" + name))
            self.engs[name] = Eng(name, h, SemCounter(sem, "s_" + name), inorder)
        self.pe, self.act, self.dve, self.pool, self.sp = (
            self.engs[k] for k in ("pe", "act", "dve", "pool", "sp"))
        self.n_dsem = 0
        self.all_dsems = []
        self.collect = None
        self.waited_ev = {}
        self.evtime = {}
        self.dma_free = 0.0
        self.LAT = 1.0
        self.streams = []
        for e in self.engs.values():
            e.free = 0.0

    def sbuf(self, name, shape, dtype):
        t = self.stack.enter_context(self.nc.sbuf_tensor(name, list(shape), dtype))
        return Buf(name, t)

    def psum(self, name, shape, dtype):
        t = self.stack.enter_context(self.nc.psum_tensor(name, list(shape), dtype))
        return Buf(name, t, is_psum=True)

    def _dsem(self, buf):
        if buf.dsem is None:
            sem = self.stack.enter_context(self.nc.semaphore("d_" + buf.name))
            buf.dsem = SemCounter(sem, "d_" + buf.name)
            self.n_dsem += 1
            self.all_dsems.append(buf.dsem)
        return buf.dsem

    def _deps(self, eng, reads, writes):
        need = {}
        def add(ev, raw):
            if ev is None:
                return
            semc, val = ev
            if semc is eng.semc:
                if eng.inorder:
                    return
                if not raw:
                    return
            if need.get(semc, 0) < val:
                need[semc] = val
        for b in reads:
            add(b.last_write, True)
            if b.is_psum:
                for ev in b.reads:
                    if ev[0] is not eng.semc:
                        add(ev, False)
        for b in writes:
            add(b.last_write, False)
            for ev in b.reads:
                add(ev, False)
        return need

    def _ready_time(self, need):
        r = 0.0
        for semc, val in need.items():
            r = max(r, self.evtime.get((semc, val), 0.0) + self.LAT)
        return r

    def est_start(self, d):
        eng = d[1]
        return max(eng.free, self._ready_time(self._deps(eng, d[3], d[4])))

    def _wait_for(self, eng, reads, writes):
        need = self._deps(eng, reads, writes)
        ready = self._ready_time(need)
        for semc, val in need.items():
            if eng.waited.get(semc, 0) >= val:
                continue
            eng.prog.append(("w", semc, val))
            self.waited_ev.setdefault(semc, set()).add(val)
            eng.waited[semc] = val
            eng.n_wait += 1
        return ready

    def _record(self, ev, reads, writes):
        for b in reads:
            b.reads.append(ev)
        for b in writes:
            b.last_write = ev
            b.reads = []

    DEFCOST = {"pe": 0.215, "act": 0.65, "dve": 0.65, "pool": 1.2, "sp": 0.1}

    def op(self, eng, fn, reads=(), writes=(), signal=True, c=None, done=None):
        d = ("op", eng, fn, tuple(reads), tuple(writes), signal, c if c is not None else self.DEFCOST[eng.name], done)
        if self.collect is not None:
            self.collect.append(d)
        else:
            self.commit(d)

    def dma(self, eng, out_ap, in_ap, reads=(), writes=(), sem_buf=None, nbytes=1 << 20, on_commit=None, **kw):
        sb = sem_buf if sem_buf is not None else (writes[0] if writes else reads[0])
        d = ("dma", eng, (out_ap, in_ap, kw), tuple(reads), tuple(writes), sb, nbytes, on_commit)
        if self.collect is not None:
            self.collect.append(d)
        else:
            self.commit(d)

    def commit(self, d):
        eng, reads, writes = d[1], d[3], d[4]
        ready = self._wait_for(eng, reads, writes)
        start = max(eng.free, ready)
        eng.n_inst += 1
        if d[0] == "op":
            fn, signal, cost = d[2], d[5], d[6]
            fin = start + cost
            eng.free = fin
            if signal:
                eng.semc.count += 1
                eng.prog.append(("i", fn, eng.semc, eng.semc.count))
                ev = (eng.semc, eng.semc.count)
            else:
                eng.prog.append(("i", fn, None, 0))
                ev = (eng.semc, eng.semc.count + 1)
            self.evtime[ev] = max(self.evtime.get(ev, 0.0), fin)
            if d[7] is not None:
                d[7]()
        else:
            (out_ap, in_ap, kw), sb, nbytes, on_commit = d[2], d[5], d[6], d[7]
            semc = self._dsem(sb)
            semc.count += 16
            eng.prog.append(("i", (lambda h, o=out_ap, i=in_ap, k=kw: h.dma_start(out=o, in_=i, **k)),
                             semc, -16))
            issue_end = start + (1.0 if eng.name == "pool" else 0.1)
            eng.free = issue_end
            xs = max(issue_end, self.dma_free)
            self.dma_free = xs + nbytes / 300e3
            ev = (semc, semc.count)
            self.evtime[ev] = self.dma_free + 2.0
            if on_commit is not None:
                on_commit()
        self._record(ev, reads, writes)

    def schedule(self, gens):
        streams = [[g, None] for g in gens]
        self.streams = streams
        idle_rounds = 0
        while streams:
            progressed = False
            for st in list(streams):
                if st[1] is None:
                    self.collect = []
                    try:
                        next(st[0])
                        st[1] = self.collect
                    except StopIteration:
                        st[1] = self.collect
                        streams.remove(st)
                        if st[1]:
                            streams.append([iter(()), st[1]])
                    self.collect = None
            cands = [st for st in streams if st[1]]
            if not cands:
                for st in streams:
                    st[1] = None
                idle_rounds += 1
                assert idle_rounds < 10000, "scheduler: all streams blocked"
                continue
            idle_rounds = 0
            best = min(cands, key=lambda st: self.est_start(st[1][0]))
            for d in best[1]:
                self.commit(d)
            best[1] = None
            for st in streams:
                if st[1] is not None and not st[1]:
                    st[1] = None

    def spawn(self, g):
        self.streams.append([g, None])

    def final_wait(self, eng, bufs):
        for b in bufs:
            if b.dsem is not None:
                eng.prog.append(("w", b.dsem, b.dsem.count))

    def emit(self):
        eng_sems = {e.semc for e in self.engs.values()}
        rank = {}
        for semc in eng_sems:
            vals = sorted(self.waited_ev.get(semc, ()))
            rank[semc] = {v: i + 1 for i, v in enumerate(vals)}
        self.n_signals = {semc.name: len(rank[semc]) for semc in eng_sems}

        def replay(eng):
            def run(h):
                for it in eng.prog:
                    if it[0] == "w":
                        semc, val = it[1], it[2]
                        if semc in eng_sems:
                            h.wait_ge(semc.sem, rank[semc][val])
                        else:
                            h.wait_ge(semc.sem, val)
                    else:
                        inst = it[1](h)
                        semc, idx = it[2], it[3]
                        if semc is None:
                            continue
                        if idx < 0:
                            inst.then_inc(semc.sem, -idx)
                        elif idx in rank[semc]:
                            inst.then_inc(semc.sem, 1)
            return run
        with self.nc.Block() as block:
            block.tensor(replay(self.pe))
            block.scalar(replay(self.act))
            block.vector(replay(self.dve))
            block.gpsimd(replay(self.pool))
            block.sync(replay(self.sp))


D = 1024
TOK = 2048
NBLK = 16
ALPHA = 2.0 ** 0.25
EPS = 1e-5
NSLOT = 6


class _Stop(Exception):
    pass


def build_nc(stop=None):
    import os
    stop = stop or os.environ.get("KSTOP")
    nc = bass.Bass("TRN2", target_bir_lowering=False)

    def ck(name):
        if stop == name:
            raise _Stop()

    def din(name, shape):
        return nc.dram_tensor(name, list(shape), F32, kind="ExternalInput").ap()

    x = din("x", [TOK + 128, D])
    p = din("p", [TOK, 256])
    w_in = din("w_in", [D, 1280])
    w_pool = din("w_pool", [4, 128, 128])
    pscale_d = din("pool_scale", [128, 4])
    sinks_d = din("attn_sinks", [128, 8])
    w_out = din("w_out", [D, D])
    g1_d = din("ln1_g", [128, D]); b1_d = din("ln1_b", [128, D])
    w_ff1 = din("w_ff1", [D, 4096]); w_ff2 = din("w_ff2", [4096, D])
    g2_d = din("ln2_g", [128, D]); b2_d = din("ln2_b", [128, D])
    w_ple = din("w_ple", [256, D]); w_gate = din("w_ple_gate", [D, D])
    bg_d = din("b_ple_gate", [1, D])
    ident_d = din("c_ident", [128, 128])
    mask_d = din("c_mask", [128, 4, 512])
    maskf_d = din("c_mask_first", [128, 4, 512])
    apool_d = din("c_apool", [128, 8, 128])
    apoolf_d = din("c_apool_first", [128, 4, 128])
    out = nc.dram_tensor("out", [TOK, D], F32, kind="ExternalOutput").ap()

    win3 = w_in.rearrange("(k p) c -> p k c", p=128)
    wo3 = w_out.rearrange("(k p) c -> p k c", p=128)
    w13 = w_ff1.rearrange("(k p) c -> p k c", p=128)
    w23 = w_ff2.rearrange("(k p) c -> p k c", p=128)
    wg3 = w_gate.rearrange("(k p) c -> p k c", p=128)
    wp3 = w_ple.rearrange("(k p) c -> p k c", p=128)

    with ExitStack() as st:
        fw = FW(nc, st)
        PE, ACT, DVE, POOL, SP = fw.pe, fw.act, fw.dve, fw.pool, fw.sp

        ident = fw.sbuf("ident", [128, 128], BF16)
        Mk = fw.sbuf("Mk", [128, 4, 512], BF16)
        Mf = fw.sbuf("Mf", [128, 4, 512], BF16)
        Ap = fw.sbuf("Ap", [128, 8, 128], BF16)
        Apf = fw.sbuf("Apf", [128, 4, 128], BF16)
        Wp = fw.sbuf("Wp", [128, 4, 128], BF16)
        G1 = fw.sbuf("G1", [128, D], F32); B1 = fw.sbuf("B1", [128, D], F32)
        G2 = fw.sbuf("G2", [128, D], F32); B2 = fw.sbuf("B2", [128, D], F32)
        esink = fw.sbuf("esink", [128, 8], F32)
        pscale = fw.sbuf("pscale", [128, 4], F32)
        bghl = fw.sbuf("bghl", [1, 2 * D], BF16)
        ones1 = fw.sbuf("ones1", [1, 128], BF16)
        cm = fw.sbuf("cm", [128, 1], F32)

        fw.dma(POOL, ident.ap[:], ident_d, writes=[ident])
        def late_consts():
            fw.dma(POOL, Ap.ap[:], apool_d, writes=[Ap])
            fw.dma(POOL, Apf.ap[:], apoolf_d, writes=[Apf])
            fw.dma(POOL, Wp.ap[:], w_pool.rearrange("g c d -> c g d"), writes=[Wp])
            fw.dma(POOL, Mk.ap[:], mask_d, writes=[Mk])
            fw.dma(POOL, Mf.ap[:], maskf_d, writes=[Mf])
        def late_consts_sp():
            fw.dma(SP, G1.ap[:], g1_d, writes=[G1], nbytes=1 << 19); fw.dma(SP, B1.ap[:], b1_d, writes=[B1], nbytes=1 << 19)
            fw.dma(SP, G2.ap[:], g2_d, writes=[G2], nbytes=1 << 19); fw.dma(SP, B2.ap[:], b2_d, writes=[B2], nbytes=1 << 19)
        fw.dma(SP, esink.ap[:], sinks_d, writes=[esink])
        fw.dma(SP, pscale.ap[:], pscale_d, writes=[pscale])
        fw.op(ACT, lambda h: h.activation(esink.ap[:], esink.ap[:], AF.Exp), reads=[esink], writes=[esink])
        fw.op(DVE, lambda h: h.memset(cm.ap[:], -0.5), writes=[cm])
        fw.op(DVE, lambda h: h.memset(ones1.ap[:], 1.0), writes=[ones1])

        xb = [fw.sbuf("xb%d" % i, [128, D], BF16) for i in range(2)]
        xf = [fw.sbuf("xf%d" % i, [128, D], F32) for i in range(2)]
        xT = fw.sbuf("xT", [128, 8, 512], BF16)
        xTh = fw.sbuf("xTh", [128, 8, 128], BF16)
        Wkd = fw.sbuf("Wkd", [128, 8, 256], BF16)
        kT = [fw.sbuf("kT%d" % i, [128, 2, 512], BF16) for i in range(2)]
        Vv = [fw.sbuf("V%d" % i, [128, 4, 130], BF16) for i in range(2)]
        uu = [fw.sbuf("u%d" % i, [128, 4, 512], BF16) for i in range(2)]
        lnb = {}
        for nm in ("A0", "A1", "C0", "C1", "C2", "C3"):
            lnb[nm] = (fw.sbuf("st" + nm, [128, 12], F32), fw.sbuf("mv" + nm, [128, 2], F32),
                       fw.sbuf("ve" + nm, [128, 1], F32), fw.sbuf("rstd" + nm, [128, 1], F32))
        dens = [fw.sbuf("den%d" % j, [128, 8], F32) for j in range(2)]
        rdens = [fw.sbuf("rden%d" % j, [128, 8], F32) for j in range(2)]
        y1 = fw.sbuf("y1", [128, D], F32)
        h1 = [fw.sbuf("h1_%d" % i, [128, D], F32) for i in range(4)]
        hbA = fw.sbuf("hbA", [128, D], BF16)
        hbC = [fw.sbuf("hbC%d" % i, [128, D], BF16) for i in range(2)]
        rl = [fw.sbuf("rl%d" % i, [128, 512], F32) for i in range(2)]
        h2 = [fw.sbuf("h2%d" % i, [128, D], F32) for i in range(2)]
        h2T = [fw.sbuf("h2T%d" % i, [128, 8, 128], BF16) for i in range(2)]
        pb = fw.sbuf("pb", [128, 4, 256], BF16)
        pT = [fw.sbuf("pT%d" % i, [128, 2, 128], BF16) for i in range(2)]
        tt = [[fw.sbuf("tt%d_%d" % (j, i), [128, 512], F32) for i in range(2)] for j in range(2)]
        ovl = st.enter_context(nc.sbuf_tensor("ovl", [128, 32 * 512], BF16))
        aT = Buf("aT", ovl)
        o = 0
        def carve(name, n):
            nonlocal o
            b = Buf(name, ovl[:, o:o + n]); o += n
            return b
        yTo = o
        yTb = [carve("yT%d" % i, 8 * 128) for i in range(4)]
        qH = carve("qH", 4 * 512); qT = carve("qT", 4 * 512)
        eo = o
        EEs = [[carve("E%d_%d" % (j, i), 512) for i in range(4)] for j in range(2)]
        dT = Buf("dT", ovl[:, eo:eo + 2048])
        yatts = [carve("yatt%d" % j, 512) for j in range(2)]
        y1b = carve("y1b", 2048)
        hbAb = carve("hbAb", 1024)
        assert o <= 32 * 512, o
        stageA = yTb + [dT, qH, qT, y1b, hbAb] + yatts + EEs[0] + EEs[1]
        aT3 = ovl[:, :].rearrange("p (c t) -> p c t", t=512)
        yT4 = ovl[:, yTo:yTo + 4096].rearrange("p (b c t) -> p b c t", b=4, c=8)
        dT3 = dT.ap.rearrange("p (c t) -> p c t", t=512)
        qT3 = qT.ap.rearrange("p (c t) -> p c t", t=512)
        qH3 = qH.ap.rearrange("p (c t) -> p c t", t=512)

        def handoff(src, dst):
            for d in dst:
                for s in src:
                    if s.last_write is not None:
                        d.reads.append(s.last_write)
                    d.reads.extend(s.reads)

        for v in Vv:
            fw.op(DVE, lambda h, v=v: h.memset(v.ap[:], 1.0), writes=[v])

        fw.dma(SP, y1.ap[0:1, :], bg_d, writes=[y1])
        fw.op(DVE, lambda h: h.tensor_copy(bghl.ap[0:1, 0:D], y1.ap[0:1, :]), reads=[y1], writes=[bghl])
        fw.op(DVE, lambda h: h.tensor_copy(h2[0].ap[0:1, :], bghl.ap[0:1, 0:D]), reads=[bghl], writes=[h2[0]])
        fw.op(DVE, lambda h: h.tensor_tensor(h2[0].ap[0:1, :], y1.ap[0:1, :], h2[0].ap[0:1, :], op=ALU.subtract),
              reads=[y1, h2[0]], writes=[h2[0]])
        fw.op(DVE, lambda h: h.tensor_copy(bghl.ap[0:1, D:2 * D], h2[0].ap[0:1, :]), reads=[h2[0]], writes=[bghl])

        banks = [fw.psum("bk%d" % i, [128, 512], F32) for i in range(8)]
        bstate = {"i": 0, "held": set()}

        def nbg(n=1):
            while True:
                free = [(bstate["i"] + j) % 8 for j in range(8) if ((bstate["i"] + j) % 8) not in bstate["held"]]
                if len(free) >= n:
                    break
                yield
            got = free[:n]
            bstate["i"] = (got[-1] + 1) % 8
            for g_ in got:
                bstate["held"].add(g_)
            return [banks[g_] for g_ in got]

        def fb(bk):
            return lambda: bstate["held"].discard(banks.index(bk))

        slots = [fw.sbuf("ws%d" % i, [128, 4096], BF16) for i in range(NSLOT)]
        pieces = []
        for t in range(4):
            pieces += [(win3[:, :, 512:1024], 8, 512), (win3[:, :, 1024:1152], 8, 128),
                       (win3[:, :, 0:512], 8, 512), (win3[:, :, 1152:1280], 8, 128),
                       (wo3[:, :, 0:512], 8, 512), (wo3[:, :, 512:1024], 8, 512)]
            pieces += [(w13[:, :, fg * 512:(fg + 1) * 512], 8, 512) for fg in range(8)]
            pieces += [(w23[:, cg * 8:(cg + 1) * 8, hf * 512:(hf + 1) * 512], 8, 512)
                       for hf in range(2) for cg in range(4)]
            pieces += [(wg3[:, :, 0:512], 8, 512), (wg3[:, :, 512:1024], 8, 512), (wp3, 2, 1024)]
        ring = {"load": 0}
        free_slots = list(range(NSLOT))
        piece_slot = {}
        piece_ok = set()

        def pview(i, sl):
            src, k, c = pieces[i]
            return slots[sl].ap[:, 0:k * c].rearrange("p (k c) -> p k c", c=c)

        def try_load(maxn=None):
            n = 0
            while free_slots and ring["load"] < len(pieces) and (maxn is None or n < maxn):
                i = ring["load"]; ring["load"] += 1
                sl = free_slots.pop(0)
                piece_slot[i] = sl
                k, c = pieces[i][1], pieces[i][2]
                fw.dma(POOL, pview(i, sl), pieces[i][0], writes=[slots[sl]], nbytes=128 * k * c * 4,
                       on_commit=lambda i=i: piece_ok.add(i))
                n += 1

        def acquire(i):
            while i not in piece_ok:
                yield
            sl = piece_slot[i]
            return slots[sl], pview(i, sl)

        def release(i):
            free_slots.append(piece_slot[i])
            try_load()

        PT = 25
        cnt = {"ev": 0}

        def evac(dst_ap, src_ap, bk, writes, eng=None, c=0.65):
            if eng is None:
                eng = DVE if (cnt["ev"] % 4 == 3) else ACT
                cnt["ev"] += 1
            if eng is ACT:
                fw.op(ACT, lambda h: h.activation(dst_ap, src_ap, AF.Copy), reads=[bk], writes=writes, c=c, done=fb(bk))
            else:
                fw.op(DVE, lambda h: h.tensor_copy(dst_ap, src_ap), reads=[bk], writes=writes, c=c, done=fb(bk))

        def transpose_to(src_buf, src_ap_fn, n, dst_buf, dst_ap, eng=None):
            (bk,) = yield from nbg()
            pv = bk.ap[:].bitcast(BF16)
            for k in range(n):
                fw.op(PE, lambda h, k=k: h.transpose(pv[:, k * 128:(k + 1) * 128], src_ap_fn(k), ident.ap[:]),
                      reads=[src_buf, ident], writes=[bk], signal=(k == n - 1), c=0.07)
            yield
            evac(dst_ap, pv[:, 0:n * 128].rearrange("p (k t) -> p k t", t=128), bk, [dst_buf], eng, c=0.2 + n * 0.07)
            yield

        def load_xb(gbl):
            s = xb[(gbl + 1) % 2]
            fw.dma(POOL, s.ap[:], x[(gbl + 1) * 128:(gbl + 2) * 128, :], writes=[s], nbytes=1 << 19)
            return s

        def load_xf(gbl):
            s = xf[gbl % 2]
            fw.dma(SP, s.ap[:], x[(gbl + 1) * 128:(gbl + 2) * 128, :], writes=[s], nbytes=1 << 19)
            return s

        def uv_block(xT_ap_fn, xT_buf, wu_b, wu, wv_b, wv, par, blk):
            bu, bv = yield from nbg(2)
            for k in range(8):
                fw.op(PE, lambda h, k=k: h.matmul(bu.ap[:], xT_ap_fn(k), wu[:, k, :], start=(k == 0), stop=(k == 7)),
                      reads=[xT_buf, wu_b], writes=[bu], signal=(k == 7))
            for k in range(8):
                fw.op(PE, lambda h, k=k: h.matmul(bv.ap[:, 0:128], xT_ap_fn(k), wv[:, k, :], start=(k == 0), stop=(k == 7)),
                      reads=[xT_buf, wv_b], writes=[bv], signal=(k == 7), c=0.07)
            yield
            evac(uu[par].ap[:, blk, :], bu.ap[:], bu, [uu[par]])
            evac(Vv[par].ap[:, blk, :].rearrange("p (h d) -> p h d", d=65)[:, :, 0:64],
                 bv.ap[:, 0:128].rearrange("p (h d) -> p h d", d=64), bv, [Vv[par]], c=0.3)
            yield

        def layernorm(nm, src, work, G, B, dst, hbuf):
            stb, mvb, veb, rsb = lnb[nm]
            (sb, sap), (wb, wap), (db, dap) = src, work, dst
            for i in range(2):
                fw.op(DVE, lambda h, i=i: h.bn_stats(stb.ap[:, i * 6:(i + 1) * 6], sap[:, i * 512:(i + 1) * 512]),
                      reads=[sb], writes=[stb], c=0.65)
            fw.op(DVE, lambda h: h.bn_aggr(mvb.ap[:], stb.ap[:]), reads=[stb], writes=[mvb], c=0.2)
            yield
            fw.op(POOL, lambda h: h.tensor_scalar(veb.ap[:], mvb.ap[:, 1:2], EPS, None, op0=ALU.add), reads=[mvb], writes=[veb], c=0.2)
            fw.op(POOL, lambda h: h.tensor_tensor(rsb.ap[:], veb.ap[:], cm.ap[:], op=ALU.pow), reads=[veb, cm], writes=[rsb], c=0.5)
            yield
            fw.op(DVE, lambda h: h.tensor_scalar(wap, sap, mvb.ap[:, 0:1], rsb.ap[:, 0:1],
                                                 op0=ALU.subtract, op1=ALU.mult), reads=[sb, mvb, rsb], writes=[wb], c=0.8)
            fw.op(DVE, lambda h: h.tensor_tensor(wap, wap, G.ap[:], op=ALU.mult), reads=[wb, G], writes=[wb], c=1.2)
            yield
            fw.op(DVE, lambda h: h.tensor_tensor(hbuf[1], wap, B.ap[:], op=ALU.add), reads=[wb, B], writes=[hbuf[0]], c=1.2)
            yield
            fw.op(POOL, lambda h: h.tensor_tensor(dap, wap, B.ap[:], op=ALU.add), reads=[wb, B], writes=[db], c=2.4)
            yield

        prog = {"c_read": -1, "attn": -1, "wln": -1, "B": -1}
        wost = {}
        attn_done = set()
        wln_cnt = {}
        st_ = {"xb_next": None}

        def gen_Apre(t):
            par = t % 2
            p0 = t * PT
            if t == 0:
                s = st_["xb_next"]
                st_["xb_next"] = load_xb(0)
                try_load(2)
                yield from transpose_to(s, lambda k, s=s: s.ap[:, k * 128:(k + 1) * 128], 8, xTh, xTh.ap[:])
            for b in range(4):
                gbl = t * 4 + b
                s = st_["xb_next"]
                if gbl + 1 < NBLK:
                    st_["xb_next"] = load_xb(gbl + 1)
                if t == 0 and b == 2:
                    try_load()
                    late_consts()
                    late_consts_sp()
                yield from transpose_to(s, lambda k, s=s: s.ap[:, k * 128:(k + 1) * 128], 8, xT, xT.ap[:, :, b * 128:(b + 1) * 128])
            handoff(EEs[0], [dT])
            fw.op(POOL, lambda h: h.memset(qT3[64:128, :, :], 0.0), writes=[qT], c=1.0)
            fw.op(POOL, lambda h: h.memset(qH3[0:64, :, :], 0.0), writes=[qH], c=1.0)
            wq_b, wq = yield from acquire(p0 + 0)
            for qc in range(4):
                (bk,) = yield from nbg()
                for k in range(8):
                    fw.op(PE, lambda h, k=k, qc=qc, bk=bk: h.matmul(bk.ap[:], wq[:, k, qc * 128:(qc + 1) * 128], xT.ap[:, k, :],
                                                                    start=(k == 0), stop=(k == 7)),
                          reads=[wq_b, xT], writes=[bk], signal=(k == 7))
                yield
                fw.op(ACT, lambda h, qc=qc, bk=bk: h.activation(qT3[0:64, qc, :], bk.ap[0:64, :], AF.Copy),
                      reads=[bk], writes=[qT], c=0.65)
                fw.op(ACT, lambda h, qc=qc, bk=bk: h.activation(qH3[64:128, qc, :], bk.ap[64:128, :], AF.Copy),
                      reads=[bk], writes=[qH], c=0.65, done=fb(bk))
                yield
            release(p0 + 0)
            wk_b, wk = yield from acquire(p0 + 1)
            for kvh in range(2):
                for r in range(2):
                    fw.op(DVE, lambda h, kvh=kvh, r=r: h.tensor_copy(Wkd.ap[:, :, kvh * 128 + r * 64:kvh * 128 + r * 64 + 64],
                                                                    wk[:, :, kvh * 64:(kvh + 1) * 64]),
                          reads=[wk_b], writes=[Wkd], c=0.4)
            yield
            release(p0 + 1)
            for kvh in range(2):
                (bk,) = yield from nbg()
                for k in range(8):
                    fw.op(PE, lambda h, k=k, kvh=kvh, bk=bk: h.matmul(bk.ap[:], Wkd.ap[:, k, kvh * 128:(kvh + 1) * 128], xT.ap[:, k, :],
                                                                      start=(k == 0), stop=(k == 7)),
                          reads=[Wkd, xT], writes=[bk], signal=(k == 7))
                yield
                evac(kT[par].ap[:, kvh, :], bk.ap[:], bk, [kT[par]])
                yield
            if t == 0:
                (bk,) = yield from nbg()
                for kvh in range(2):
                    for k in range(8):
                        fw.op(PE, lambda h, k=k, kvh=kvh, bk=bk: h.matmul(bk.ap[:, kvh * 128:(kvh + 1) * 128],
                                                                          Wkd.ap[:, k, kvh * 128:(kvh + 1) * 128], xTh.ap[:, k, :],
                                                                          start=(k == 0), stop=(k == 7)),
                              reads=[Wkd, xTh], writes=[bk], signal=(k == 7), c=0.07)
                yield
                evac(kT[1].ap[:, :, 384:512], bk.ap[:, 0:256].rearrange("p (h t) -> p h t", t=128), bk, [kT[1]], c=0.4)
                yield
            wu_b, wu = yield from acquire(p0 + 2)
            wv_b, wv = yield from acquire(p0 + 3)
            if t == 0:
                yield from uv_block(lambda k: xTh.ap[:, k, :], xTh, wu_b, wu, wv_b, wv, 1, 3)
            for b in range(4):
                yield from uv_block(lambda k, b=b: xT.ap[:, k, b * 128:(b + 1) * 128], xT, wu_b, wu, wv_b, wv, par, b)
            release(p0 + 2); release(p0 + 3)
            for b in range(4):
                first = (t == 0 and b == 0)
                if b == 0:
                    up_b, up = uu[1 - par], uu[1 - par].ap[:, 3, :]
                else:
                    up_b, up = uu[par], uu[par].ap[:, b - 1, :]
                uc = uu[par].ap[:, b, :]
                (bk,) = yield from nbg()
                for g in range(4):
                    acur_b, acur = (Apf, Apf.ap[:, g, :]) if first else (Ap, Ap.ap[:, 4 + g, :])
                    fw.op(PE, lambda h, g=g, bk=bk, up=up: h.matmul(bk.ap[:, g * 128:(g + 1) * 128], up[:, g * 128:(g + 1) * 128],
                                                                    Ap.ap[:, g, :], start=True, stop=False),
                          reads=[up_b, Ap], writes=[bk], signal=False, c=0.07)
                    fw.op(PE, lambda h, g=g, bk=bk, uc=uc, acur=acur: h.matmul(bk.ap[:, g * 128:(g + 1) * 128], uc[:, g * 128:(g + 1) * 128],
                                                                                 acur, start=False, stop=True),
                          reads=[uu[par], acur_b], writes=[bk], signal=(g == 3), c=0.07)
                yield
                evac(dT3[:, :, b * 128:(b + 1) * 128], bk.ap[:].rearrange("p (g t) -> p g t", t=128), bk, [dT])
                yield
            for g in range(4):
                (bk,) = yield from nbg()
                fw.op(PE, lambda h, g=g, bk=bk: h.matmul(bk.ap[:], Wp.ap[:, g, :], dT3[:, g, :], start=True, stop=True),
                      reads=[Wp, dT], writes=[bk])
                yield
                fw.op(ACT, lambda h, g=g, bk=bk: h.activation(yT4[:, :, g, :], bk.ap[:].rearrange("p (b t) -> p b t", t=128),
                                                              AF.Copy, scale=pscale.ap[:, g:g + 1]),
                      reads=[bk, pscale], writes=yTb, done=fb(bk))
                yield
            handoff([dT], EEs[0])
            wo0_b, wo0 = yield from acquire(p0 + 4)
            wo1_b, wo1 = yield from acquire(p0 + 5)
            wost[t] = [(wo0_b, wo0), (wo1_b, wo1)]

        def gen_attn(t, s2):
            par = t % 2
            EE = EEs[s2]; yatt = yatts[s2]; den = dens[s2]; rden = rdens[s2]
            for b in (s2, s2 + 2):
                gbl = t * 4 + b
                first = (gbl == 0)
                if b == 0:
                    kp_b, kp = kT[1 - par], kT[1 - par].ap[:, :, 384:512]
                    vp_b, vp = Vv[1 - par], Vv[1 - par].ap[:, 3, :]
                else:
                    kp_b, kp = kT[par], kT[par].ap[:, :, (b - 1) * 128:b * 128]
                    vp_b, vp = Vv[par], Vv[par].ap[:, b - 1, :]
                kc = kT[par].ap[:, :, b * 128:(b + 1) * 128]
                vc = Vv[par].ap[:, b, :]
                for kvh in range(2):
                    for half in range(2):
                        (bk,) = yield from nbg()
                        for kb in range(2):
                            kk_b, kk = (kp_b, kp) if kb == 0 else (kT[par], kc)
                            for gp in range(2):
                                qc = 2 * kvh + gp
                                col = (kb * 2 + gp) * 128
                                qz_b, qz3 = (qT, qT3) if half == 0 else (qH, qH3)
                                fw.op(PE, lambda h, bk=bk, col=col, kk=kk, kvh=kvh, qc=qc, qz3=qz3, b=b:
                                      h.matmul(bk.ap[:, col:col + 128], kk[:, kvh, :],
                                               qz3[:, qc, b * 128:(b + 1) * 128], start=True, stop=True),
                                      reads=[kk_b, qz_b], writes=[bk], signal=(kb == 1 and gp == 1), c=0.07)
                        yield
                        E = EE[kvh * 2 + half]
                        fw.op(ACT, lambda h, bk=bk, E=E: h.activation(E.ap, bk.ap[:], AF.Exp, scale=0.125), reads=[bk], writes=[E],
                              c=0.5, done=fb(bk))
                        yield
                        m_b = Mf if first else Mk
                        m = m_b.ap[:, kvh * 2 + half, :]
                        fw.op(DVE, lambda h, E=E, m=m: h.tensor_tensor(E.ap, E.ap, m, op=ALU.mult), reads=[E, m_b], writes=[E], c=0.43)
                        yield
                bos = yield from nbg(2)
                for kvh in range(2):
                    bo = bos[kvh]
                    for g in range(4):
                        gp, half = g // 2, g % 2
                        E = EE[kvh * 2 + half]
                        for kb in range(2):
                            col = (kb * 2 + gp) * 128
                            vv_b, vv = (vp_b, vp) if kb == 0 else (Vv[par], vc)
                            fw.op(PE, lambda h, bo=bo, g=g, E=E, vv=vv, kvh=kvh, kb=kb, col=col:
                                  h.matmul(bo.ap[:, g * 65:(g + 1) * 65], E.ap[:, col:col + 128],
                                           vv[:, kvh * 65:(kvh + 1) * 65], start=(kb == 0), stop=(kb == 1)),
                                  reads=[E, vv_b], writes=[bo], signal=(g == 3 and kb == 1), c=0.07)
                    yield
                for kvh in range(2):
                    bo = bos[kvh]
                    bo3 = bo.ap[:, 0:260].rearrange("p (h d) -> p h d", d=65)
                    fw.op(DVE, lambda h, bo3=bo3, kvh=kvh: h.tensor_tensor(den.ap[:, kvh * 4:(kvh + 1) * 4], bo3[:, :, 64],
                                                                           esink.ap[:, kvh * 4:(kvh + 1) * 4], op=ALU.add),
                          reads=[bo, esink], writes=[den], c=0.15)
                    fw.op(DVE, lambda h, kvh=kvh: h.reciprocal(rden.ap[:, kvh * 4:(kvh + 1) * 4], den.ap[:, kvh * 4:(kvh + 1) * 4]),
                          reads=[den], writes=[rden], c=0.2)
                    yield
                    for g in range(4):
                        hh = 4 * kvh + g
                        dn = fb(bo) if g == 3 else None
                        if False:
                            fw.op(DVE, lambda h, bo3=bo3, g=g, hh=hh: h.tensor_scalar(yatt.ap[:, hh * 64:(hh + 1) * 64], bo3[:, g, 0:64],
                                                                                      rden.ap[:, hh:hh + 1], None, op0=ALU.mult),
                                  reads=[bo, rden], writes=[yatt], c=0.22, done=dn)
                        else:
                            fw.op(ACT, lambda h, bo3=bo3, g=g, hh=hh: h.activation(yatt.ap[:, hh * 64:(hh + 1) * 64], bo3[:, g, 0:64],
                                                                                   AF.Copy, scale=rden.ap[:, hh:hh + 1]),
                                  reads=[bo, rden], writes=[yatt], c=0.32, done=dn)
                    yield
                yield from transpose_to(yatt, lambda k: yatt.ap[:, k * 128:(k + 1) * 128], 4, yTb[b], yT4[:, b, 4:8, :])
                attn_done.add(gbl)

        def gen_wln(t, s2):
            wo = wost[t]
            p0 = t * PT
            if s2 == 0:
                y1_b, y1_ap, hb_b, hb_ap = y1, y1.ap[:], hbA, hbA.ap[:]
            else:
                y1_b, y1_ap, hb_b, hb_ap = y1b, y1b.ap.bitcast(F32), hbAb, hbAb.ap
            xfs = load_xf(t * 4 + s2)
            for b in (s2, s2 + 2):
                gbl = t * 4 + b
                while gbl not in attn_done:
                    yield
                for hf in range(2):
                    (bk,) = yield from nbg()
                    wb_, wv_ = wo[hf]
                    for c in range(8):
                        fw.op(PE, lambda h, bk=bk, c=c, wv_=wv_, b=b: h.matmul(bk.ap[:], yT4[:, b, c, :], wv_[:, c, :],
                                                                              start=(c == 0), stop=(c == 7)),
                              reads=[yTb[b], wb_], writes=[bk], signal=(c == 7))
                    yield
                    fw.op(DVE, lambda h, bk=bk, hf=hf, xfs=xfs: h.scalar_tensor_tensor(y1_ap[:, hf * 512:(hf + 1) * 512],
                                                                                       xfs.ap[:, hf * 512:(hf + 1) * 512], ALPHA, bk.ap[:],
                                                                                       op0=ALU.mult, op1=ALU.add),
                          reads=[xfs, bk], writes=[y1_b], c=0.7, done=fb(bk))
                    yield
                if b == s2:
                    xfs = load_xf(gbl + 2)
                while gbl >= 4 and (gbl - 4) not in cst["read"]:
                    yield
                yield from layernorm("A%d" % s2, (y1_b, y1_ap), (y1_b, y1_ap), G1, B1, (h1[b], h1[b].ap[:]), (hb_b, hb_ap))
                yield from transpose_to(hb_b, lambda k: hb_ap[:, k * 128:(k + 1) * 128], 8, xT, xT.ap[:, :, b * 128:(b + 1) * 128])
            wln_cnt[t] = wln_cnt.get(t, 0) + 1
            if wln_cnt[t] == 2:
                release(p0 + 4); release(p0 + 5)
                yield
                prog["wln"] = t

        def gen_B(t):
            p0 = t * PT
            handoff(stageA, [aT])
            for fg in range(8):
                w1_b, w1 = yield from acquire(p0 + 6 + fg)
                for fc in range(4):
                    c = fg * 4 + fc
                    (bk,) = yield from nbg()
                    for k in range(8):
                        fw.op(PE, lambda h, bk=bk, k=k, fc=fc, w1=w1: h.matmul(bk.ap[:], w1[:, k, fc * 128:(fc + 1) * 128], xT.ap[:, k, :],
                                                                              start=(k == 0), stop=(k == 7)),
                              reads=[w1_b, xT], writes=[bk], signal=(k == 7))
                    yield
                    r = rl[c % 2]
                    fw.op(ACT, lambda h, bk=bk, r=r: h.activation(r.ap[:], bk.ap[:], AF.Relu), reads=[bk], writes=[r], c=0.6, done=fb(bk))
                    yield
                    fw.op(DVE, lambda h, r=r, c=c: h.tensor_tensor(aT3[:, c, :], r.ap[:], r.ap[:], op=ALU.mult), reads=[r], writes=[aT], c=0.62)
                    yield
                release(p0 + 6 + fg)
            for hf in range(2):
                acc = yield from nbg(4)
                for cg in range(4):
                    w2_b, w2 = yield from acquire(p0 + 14 + hf * 4 + cg)
                    for j in range(8):
                        c = cg * 8 + j
                        for b in range(4):
                            fw.op(PE, lambda h, c=c, j=j, b=b, w2=w2, acc=acc: h.matmul(acc[b].ap[:], aT3[:, c, b * 128:(b + 1) * 128], w2[:, j, :],
                                                                                        start=(c == 0), stop=(c == 31)),
                                  reads=[aT, w2_b], writes=[acc[b]], signal=(c == 31 or (j == 7 and b == 3)))
                        if j % 2 == 1:
                            yield
                    release(p0 + 14 + hf * 4 + cg)
                for b in range(4):
                    fw.op(DVE, lambda h, b=b, hf=hf, acc=acc: h.scalar_tensor_tensor(h1[b].ap[:, hf * 512:(hf + 1) * 512],
                                                                                     h1[b].ap[:, hf * 512:(hf + 1) * 512], ALPHA, acc[b].ap[:],
                                                                                     op0=ALU.mult, op1=ALU.add),
                          reads=[h1[b], acc[b]], writes=[h1[b]], c=0.7, done=fb(acc[b]))
                    yield
            handoff([aT], stageA)
            prog["B"] = t

        cst = {"done": [-1, -1, -1, -1], "read": set(), "pb": -1, "rel": {}}
        cbufs = {cp_: dict(hs=h2[cp_], hbc=hbC[cp_], hT=h2T[cp_], pT=pT[cp_], tt=tt[cp_]) for cp_ in range(2)}
        oo = 0
        for cp_ in (2, 3):
            def cv(name, n, f32=False, r3=None):
                nonlocal oo
                apv = ovl[:, oo:oo + n]
                oo += n
                if f32:
                    apv = apv.bitcast(F32)
                if r3:
                    apv = apv.rearrange("p (k t) -> p k t", t=r3)
                return Buf(name + str(cp_), apv)
            cbufs[cp_] = dict(hs=cv("xh2_", 2048, f32=True), hbc=cv("xhbC_", 1024), hT=cv("xh2T_", 1024, r3=128),
                              pT=cv("xpT_", 256, r3=128), tt=[cv("xtt0_", 1024, f32=True), cv("xtt1_", 1024, f32=True)])

        def gen_C(t, cp, blocks, nstreams):
            p0 = t * PT
            cb = cbufs[cp]
            wg0_b, wg0 = yield from acquire(p0 + 22)
            wg1_b, wg1 = yield from acquire(p0 + 23)
            wpl_b, wpl = yield from acquire(p0 + 24)
            wgs = [(wg0_b, wg0), (wg1_b, wg1)]
            if cp == 0:
                while cst["done"][1] < t - 1:
                    yield
                fw.dma(POOL, pb.ap[:], p[t * 512:(t + 1) * 512, :].rearrange("(b q) c -> q b c", q=128), writes=[pb], nbytes=1 << 19)
                yield
                cst["pb"] = t
            else:
                while cst["pb"] < t:
                    yield
            for b in blocks:
                gbl = t * 4 + b
                hs = cb["hs"]
                hbc = cb["hbc"]
                yield from layernorm("C%d" % cp, (h1[b], h1[b].ap[:]), (hs, hs.ap[:]), G2, B2, (hs, hs.ap[:]), (hbc, hbc.ap[:]))
                cst["read"].add(gbl)
                hT = cb["hT"]
                yield from transpose_to(hbc, lambda k, hbc=hbc: hbc.ap[:, k * 128:(k + 1) * 128], 8, hT, hT.ap[:])
                pTs = cb["pT"]
                yield from transpose_to(pb, lambda k, b=b: pb.ap[:, b, k * 128:(k + 1) * 128], 2, pTs, pTs.ap[:])
                for hf in range(2):
                    bg_, bp_ = yield from nbg(2)
                    wb_, wv_ = wgs[hf]
                    for k in range(8):
                        fw.op(PE, lambda h, bg_=bg_, k=k, wv_=wv_, hT=hT: h.matmul(bg_.ap[:], hT.ap[:, k, :], wv_[:, k, :], start=(k == 0), stop=False),
                              reads=[hT, wb_], writes=[bg_], signal=False)
                    for r in range(2):
                        fw.op(PE, lambda h, bg_=bg_, r=r, hf=hf: h.matmul(bg_.ap[:], ones1.ap[0:1, :],
                                                                          bghl.ap[0:1, r * D + hf * 512:r * D + (hf + 1) * 512],
                                                                          start=False, stop=(r == 1)),
                              reads=[ones1, bghl], writes=[bg_], signal=(r == 1))
                    for k in range(2):
                        fw.op(PE, lambda h, bp_=bp_, k=k, hf=hf, pTs=pTs: h.matmul(bp_.ap[:], pTs.ap[:, k, :], wpl[:, k, hf * 512:(hf + 1) * 512],
                                                                                   start=(k == 0), stop=(k == 1)),
                              reads=[pTs, wpl_b], writes=[bp_], signal=(k == 1))
                    yield
                    tts = cb["tt"][hf]
                    fw.op(ACT, lambda h, bg_=bg_, tts=tts: h.activation(tts.ap[:], bg_.ap[:], AF.Tanh, scale=0.5), reads=[bg_], writes=[tts],
                          c=0.6, done=fb(bg_))
                    yield
                    fw.op(DVE, lambda h, tts=tts, bp_=bp_: h.scalar_tensor_tensor(tts.ap[:], tts.ap[:], 1.0, bp_.ap[:], op0=ALU.add, op1=ALU.mult),
                          reads=[tts, bp_], writes=[tts], c=0.7, done=fb(bp_))
                    fw.op(DVE, lambda h, tts=tts, hs=hs, hf=hf: h.scalar_tensor_tensor(hs.ap[:, hf * 512:(hf + 1) * 512], tts.ap[:], 0.5,
                                                                                       hs.ap[:, hf * 512:(hf + 1) * 512], op0=ALU.mult, op1=ALU.add),
                          reads=[tts, hs], writes=[hs], c=0.7)
                    yield
                fw.dma(SP, out[gbl * 128:(gbl + 1) * 128, :], hs.ap[:], reads=[hs], sem_buf=hs, nbytes=1 << 19)
                yield
            cst["done"][cp] = t
            cst["rel"][t] = cst["rel"].get(t, 0) + 1
            if cst["rel"][t] == nstreams:
                release(p0 + 22); release(p0 + 23); release(p0 + 24)
                yield

        def main_line():
            for t in range(4):
                yield from gen_Apre(t)
                fw.spawn(gen_attn(t, 0))
                fw.spawn(gen_attn(t, 1))
                fw.spawn(gen_wln(t, 0))
                fw.spawn(gen_wln(t, 1))
                while prog["wln"] < t:
                    yield
                yield from gen_B(t)

        def c_line(cp):
            for t in range(4):
                while prog["B"] < t:
                    yield
                if t < 3:
                    yield from gen_C(t, cp, (cp, cp + 2), 2)
                else:
                    yield from gen_C(t, cp, (cp,), 4)

        def c_line_x(cp):
            while prog["B"] < 3:
                yield
            handoff([aT], [v for v in cbufs[cp].values() if isinstance(v, Buf)] + cbufs[cp]["tt"])
            yield from gen_C(3, cp, (cp,), 4)

        st_["xb_next"] = load_xb(-1)
        fw.schedule([main_line(), c_line(0), c_line(1), c_line_x(2), c_line_x(3)])
        fw.final_wait(SP, h2 + [cbufs[2]["hs"], cbufs[3]["hs"]])
        fw.emit()
        build_nc.stats = {e.name: (e.n_inst, e.n_wait, round(e.free, 1)) for e in fw.engs.values()}
    return nc


def _constants():
    j = np.arange(128)[:, None]
    q = np.arange(128)[None, :]
    mask = np.zeros((128, 2, 2, 2, 2, 128), np.float32)
    for kvh in range(2):
        for half in range(2):
            for gp in range(2):
                slope = 2.0 ** (-(4 * kvh + 2 * gp + half + 1))
                dprev = (q + 128 - j).astype(np.float32)
                dcur = (q - j).astype(np.float32)
                mask[:, kvh, half, 0, gp, :] = np.where(j > q, np.exp(-slope * dprev), 0.0)
                mask[:, kvh, half, 1, gp, :] = np.where(j <= q, np.exp(-slope * dcur), 0.0)
    apool = np.zeros((128, 2, 4, 128), np.float32)
    apool_first = np.zeros((128, 4, 128), np.float32)
    si = np.arange(128)[:, None]
    ti = np.arange(128)[None, :]
    for g, w in enumerate((2, 4, 8, 16)):
        apool[:, 0, g, :] = np.where(128 + ti - si < w, 1.0 / w, 0.0)
        cur = np.where((si <= ti) & (ti - si < w), 1.0 / w, 0.0)
        apool[:, 1, g, :] = cur - (si == ti)
        cnt = np.minimum(ti + 1, w).astype(np.float32)
        apool_first[:, g, :] = np.where((si <= ti) & (ti - si < w), 1.0 / cnt, 0.0) - (si == ti)
    return mask, apool, apool_first


def _make_in_maps(x, p, w_in, w_pool, pool_scale, attn_sinks, w_out, ln1_g, ln1_b,
                  w_ff1, w_ff2, ln2_g, ln2_b, w_ple, w_ple_gate, b_ple_gate):
    f = lambda a: np.ascontiguousarray(np.asarray(a, dtype=np.float32))
    x = f(x); p = f(p)
    mask, apool, apool_first = _constants()
    rep = lambda v: np.ascontiguousarray(np.broadcast_to(f(v).reshape(1, -1), (128, f(v).size)))
    shared = {
        "w_in": f(w_in)[0], "w_pool": f(w_pool)[0],
        "pool_scale": np.ascontiguousarray(f(pool_scale)[0].reshape(4, 128).T),
        "attn_sinks": rep(attn_sinks[0]), "w_out": f(w_out)[0],
        "ln1_g": rep(ln1_g[0]), "ln1_b": rep(ln1_b[0]), "w_ff1": f(w_ff1)[0], "w_ff2": f(w_ff2)[0],
        "ln2_g": rep(ln2_g[0]), "ln2_b": rep(ln2_b[0]), "w_ple": f(w_ple)[0], "w_ple_gate": f(w_ple_gate)[0],
        "b_ple_gate": f(b_ple_gate)[0].reshape(1, D),
        "c_ident": np.eye(128, dtype=np.float32),
        "c_mask": np.ascontiguousarray(mask.reshape(128, 4, 512)),
        "c_apool": np.ascontiguousarray(apool.reshape(128, 8, 128)),
    }
    in_maps = []
    for c in range(8):
        bi, j = c // 4, c % 4
        xs = np.zeros((TOK + 128, D), np.float32)
        lo = j * TOK
        xs[128:] = x[bi, lo:lo + TOK]
        if j > 0:
            xs[:128] = x[bi, lo - 128:lo]
        m = dict(shared)
        m["x"] = xs
        m["p"] = np.ascontiguousarray(p[0, bi, lo:lo + TOK])
        mf = mask.copy()
        if j == 0:
            mf[:, :, :, 0] = 0.0
            m["c_apool_first"] = np.ascontiguousarray(apool_first)
        else:
            m["c_apool_first"] = np.ascontiguousarray(apool[:, 1])
        m["c_mask_first"] = np.ascontiguousarray(mf.reshape(128, 4, 512))
        in_maps.append(m)
    return in_maps


def kernel(**inputs):
    in_maps = _make_in_maps(**inputs)
    nc = build_nc()
    res = run_bass_kernel_spmd(nc, in_maps, core_ids=list(range(8)))
    outp = np.zeros((2, 8192, D), np.float32)
    for c in range(8):
        bi, j = c // 4, c % 4
        outp[bi, j * TOK:(j + 1) * TOK] = res.results[c]["out"]
    return outp
```

```python
import numpy as np
import concourse.bass as bass
import concourse.mybir as mybir
from contextlib import ExitStack
from concourse.bass_utils import run_bass_kernel_spmd

F32 = mybir.dt.float32
BF16 = mybir.dt.bfloat16
AF = mybir.ActivationFunctionType
ALU = mybir.AluOpType


import os
import random
SCHED_SEED = int(os.environ.get("KSEED", "1"))


class Buf:
    __slots__ = ("name", "ap", "last_write", "reads", "dsem", "is_psum")

    def __init__(self, name, ap=None, is_psum=False):
        self.name = name
        self.ap = ap
        self.is_psum = is_psum
        self.last_write = None
        self.reads = []
        self.dsem = None


class SemCounter:
    __slots__ = ("sem", "count", "name", "owner")

    def __init__(self, sem, name, owner=None):
        self.sem = sem
        self.count = 0
        self.name = name
        self.owner = owner


class Eng:
    def __init__(self, name, h, semc, inorder):
        self.name = name
        self.h = h
        self.semc = semc
        semc.owner = self
        self.waited = {}
        self.inorder = inorder
        self.n_wait = 0
        self.n_inst = 0
        self.prog = []


class FW:
    def __init__(self, nc, stack):
        self.nc = nc
        self.stack = stack
        self.engs = {}
        for name, h, inorder in (("pe", nc.tensor, True), ("act", nc.scalar, False),
                                 ("dve", nc.vector, False), ("pool", nc.gpsimd, False),
                                 ("sp", nc.sync, False)):
            sem = stack.enter_context(nc.semaphore("s_" + name))
            self.engs[name] = Eng(name, h, SemCounter(sem, "s_" + name), inorder)
        self.pe, self.act, self.dve, self.pool, self.sp = (
            self.engs[k] for k in ("pe", "act", "dve", "pool", "sp"))
        self.n_dsem = 0
        self.all_dsems = []
        self.collect = None
        self.jit = random.Random(SCHED_SEED) if SCHED_SEED is not None else None
        self.waited_ev = {}
        self.evtime = {}
        self.dma_free = 0.0
        self.LAT = 1.0
        self.streams = []
        for e in self.engs.values():
            e.free = 0.0

    def sbuf(self, name, shape, dtype):
        t = self.stack.enter_context(self.nc.sbuf_tensor(name, list(shape), dtype))
        return Buf(name, t)

    def psum(self, name, shape, dtype):
        t = self.stack.enter_context(self.nc.psum_tensor(name, list(shape), dtype))
        return Buf(name, t, is_psum=True)

    def _dsem(self, buf):
        if buf.dsem is None:
            sem = self.stack.enter_context(self.nc.semaphore("d_" + buf.name))
            buf.dsem = SemCounter(sem, "d_" + buf.name)
            self.n_dsem += 1
            self.all_dsems.append(buf.dsem)
        return buf.dsem

    def _deps(self, eng, reads, writes):
        need = {}
        def add(ev, raw):
            if ev is None:
                return
            semc, val = ev
            if semc is eng.semc:
                if eng.inorder:
                    return
                if not raw:
                    return
            if need.get(semc, 0) < val:
                need[semc] = val
        for b in reads:
            add(b.last_write, True)
            if b.is_psum:
                for ev in b.reads:
                    if ev[0] is not eng.semc:
                        add(ev, False)
        for b in writes:
            add(b.last_write, False)
            for ev in b.reads:
                add(ev, False)
        return need

    def _ready_time(self, need):
        r = 0.0
        for semc, val in need.items():
            r = max(r, self.evtime.get((semc, val), 0.0) + self.LAT)
        return r

    def est_start(self, d):
        eng = d[1]
        return max(eng.free, self._ready_time(self._deps(eng, d[3], d[4])))

    def _wait_for(self, eng, reads, writes):
        need = self._deps(eng, reads, writes)
        ready = self._ready_time(need)
        for semc, val in need.items():
            if eng.waited.get(semc, 0) >= val:
                continue
            eng.prog.append(("w", semc, val))
            self.waited_ev.setdefault(semc, set()).add(val)
            eng.waited[semc] = val
            eng.n_wait += 1
        return ready

    def _record(self, ev, reads, writes):
        for b in reads:
            b.reads.append(ev)
        for b in writes:
            b.last_write = ev
            b.reads = []

    DEFCOST = {"pe": 0.215, "act": 0.65, "dve": 0.65, "pool": 1.2, "sp": 0.1}

    def op(self, eng, fn, reads=(), writes=(), signal=True, c=None, done=None):
        c = c if c is not None else self.DEFCOST[eng.name]
        if self.jit is not None:
            c *= 1.0 + 0.3 * (self.jit.random() - 0.5)
        d = ("op", eng, fn, tuple(reads), tuple(writes), signal, c, done)
        if self.collect is not None:
            self.collect.append(d)
        else:
            self.commit(d)

    def dma(self, eng, out_ap, in_ap, reads=(), writes=(), sem_buf=None, nbytes=1 << 20, on_commit=None, **kw):
        sb = sem_buf if sem_buf is not None else (writes[0] if writes else reads[0])
        d = ("dma", eng, (out_ap, in_ap, kw), tuple(reads), tuple(writes), sb, nbytes, on_commit)
        if self.collect is not None:
            self.collect.append(d)
        else:
            self.commit(d)

    def commit(self, d):
        eng, reads, writes = d[1], d[3], d[4]
        ready = self._wait_for(eng, reads, writes)
        start = max(eng.free, ready)
        eng.n_inst += 1
        if d[0] == "op":
            fn, signal, cost = d[2], d[5], d[6]
            fin = start + cost
            eng.free = fin
            if signal:
                eng.semc.count += 1
                eng.prog.append(("i", fn, eng.semc, eng.semc.count))
                ev = (eng.semc, eng.semc.count)
            else:
                eng.prog.append(("i", fn, None, 0))
                ev = (eng.semc, eng.semc.count + 1)
            self.evtime[ev] = max(self.evtime.get(ev, 0.0), fin)
            if d[7] is not None:
                d[7]()
        else:
            (out_ap, in_ap, kw), sb, nbytes, on_commit = d[2], d[5], d[6], d[7]
            semc = self._dsem(sb)
            semc.count += 16
            eng.prog.append(("i", (lambda h, o=out_ap, i=in_ap, k=kw: h.dma_start(out=o, in_=i, **k)),
                             semc, -16))
            issue_end = start + (1.0 if eng.name == "pool" else 0.1)
            eng.free = issue_end
            xs = max(issue_end, self.dma_free)
            self.dma_free = xs + nbytes / 300e3
            ev = (semc, semc.count)
            self.evtime[ev] = self.dma_free + 2.0
            if on_commit is not None:
                on_commit()
        self._record(ev, reads, writes)

    def schedule(self, gens):
        streams = [[g, None] for g in gens]
        self.streams = streams
        idle_rounds = 0
        while streams:
            progressed = False
            for st in list(streams):
                if st[1] is None:
                    self.collect = []
                    try:
                        next(st[0])
                        st[1] = self.collect
                    except StopIteration:
                        st[1] = self.collect
                        streams.remove(st)
                        if st[1]:
                            streams.append([iter(()), st[1]])
                    self.collect = None
            cands = [st for st in streams if st[1]]
            if not cands:
                for st in streams:
                    st[1] = None
                idle_rounds += 1
                assert idle_rounds < 10000, "scheduler: all streams blocked"
                continue
            idle_rounds = 0
            best = min(cands, key=lambda st: self.est_start(st[1][0]))
            for d in best[1]:
                self.commit(d)
            best[1] = None
            for st in streams:
                if st[1] is not None and not st[1]:
                    st[1] = None

    def spawn(self, g):
        self.streams.append([g, None])

    def final_wait(self, eng, bufs):
        for b in bufs:
            if b.dsem is not None:
                eng.prog.append(("w", b.dsem, b.dsem.count))

    def emit(self):
        eng_sems = {e.semc for e in self.engs.values()}
        rank = {}
        for semc in eng_sems:
            vals = sorted(self.waited_ev.get(semc, ()))
            rank[semc] = {v: i + 1 for i, v in enumerate(vals)}
        self.n_signals = {semc.name: len(rank[semc]) for semc in eng_sems}

        def replay(eng):
            def run(h):
                for it in eng.prog:
                    if it[0] == "w":
                        semc, val = it[1], it[2]
                        if semc in eng_sems:
                            h.wait_ge(semc.sem, rank[semc][val])
                        else:
                            h.wait_ge(semc.sem, val)
                    else:
                        inst = it[1](h)
                        semc, idx = it[2], it[3]
                        if semc is None:
                            continue
                        if idx < 0:
                            inst.then_inc(semc.sem, -idx)
                        elif idx in rank[semc]:
                            inst.then_inc(semc.sem, 1)
            return run
        with self.nc.Block() as block:
            block.tensor(replay(self.pe))
            block.scalar(replay(self.act))
            block.vector(replay(self.dve))
            block.gpsimd(replay(self.pool))
            block.sync(replay(self.sp))


D = 1024
TOK = 2048
NBLK = 16
ALPHA = 2.0 ** 0.25
EPS = 1e-5
NSLOT = 6


class _Stop(Exception):
    pass


def build_nc(stop=None):
    import os
    stop = stop or os.environ.get("KSTOP")
    nc = bass.Bass("TRN2", target_bir_lowering=False)

    def ck(name):
        if stop == name:
            raise _Stop()

    def din(name, shape):
        return nc.dram_tensor(name, list(shape), F32, kind="ExternalInput").ap()

    x = din("x", [TOK + 128, D])
    p = din("p", [TOK, 256])
    w_in = din("w_in", [D, 1280])
    w_pool = din("w_pool", [4, 128, 128])
    pscale_d = din("pool_scale", [128, 4])
    sinks_d = din("attn_sinks", [128, 8])
    w_out = din("w_out", [D, D])
    g1_d = din("ln1_g", [128, D]); b1_d = din("ln1_b", [128, D])
    w_ff1 = din("w_ff1", [D, 4096]); w_ff2 = din("w_ff2", [4096, D])
    g2_d = din("ln2_g", [128, D]); b2_d = din("ln2_b", [128, D])
    w_ple = din("w_ple", [256, D]); w_gate = din("w_ple_gate", [D, D])
    bg_d = din("b_ple_gate", [1, D])
    ident_d = din("c_ident", [128, 128])
    mask_d = din("c_mask", [128, 4, 512])
    maskf_d = din("c_mask_first", [128, 4, 512])
    apool_d = din("c_apool", [128, 8, 128])
    apoolf_d = din("c_apool_first", [128, 4, 128])
    out = nc.dram_tensor("out", [TOK, D], F32, kind="ExternalOutput").ap()

    win3 = w_in.rearrange("(k p) c -> p k c", p=128)
    wo3 = w_out.rearrange("(k p) c -> p k c", p=128)
    w13 = w_ff1.rearrange("(k p) c -> p k c", p=128)
    w23 = w_ff2.rearrange("(k p) c -> p k c", p=128)
    wg3 = w_gate.rearrange("(k p) c -> p k c", p=128)
    wp3 = w_ple.rearrange("(k p) c -> p k c", p=128)

    with ExitStack() as st:
        fw = FW(nc, st)
        PE, ACT, DVE, POOL, SP = fw.pe, fw.act, fw.dve, fw.pool, fw.sp

        ident = fw.sbuf("ident", [128, 128], BF16)
        Mk = fw.sbuf("Mk", [128, 4, 512], BF16)
        Mf = fw.sbuf("Mf", [128, 4, 512], BF16)
        Ap = fw.sbuf("Ap", [128, 8, 128], BF16)
        Apf = fw.sbuf("Apf", [128, 4, 128], BF16)
        Wp = fw.sbuf("Wp", [128, 4, 128], BF16)
        G1 = fw.sbuf("G1", [128, D], F32); B1 = fw.sbuf("B1", [128, D], F32)
        G2 = fw.sbuf("G2", [128, D], F32); B2 = fw.sbuf("B2", [128, D], F32)
        esink = fw.sbuf("esink", [128, 8], F32)
        pscale = fw.sbuf("pscale", [128, 4], F32)
        bghl = fw.sbuf("bghl", [1, 2 * D], BF16)
        ones1 = fw.sbuf("ones1", [1, 128], BF16)
        cm = fw.sbuf("cm", [128, 1], F32)

        fw.dma(POOL, ident.ap[:], ident_d, writes=[ident])
        def late_consts():
            fw.dma(POOL, Ap.ap[:], apool_d, writes=[Ap])
            fw.dma(POOL, Apf.ap[:], apoolf_d, writes=[Apf])
            fw.dma(POOL, Wp.ap[:], w_pool.rearrange("g c d -> c g d"), writes=[Wp])
            fw.dma(POOL, Mk.ap[:], mask_d, writes=[Mk])
            fw.dma(POOL, Mf.ap[:], maskf_d, writes=[Mf])
        def late_consts_sp():
            fw.dma(SP, G1.ap[:], g1_d, writes=[G1], nbytes=1 << 19); fw.dma(SP, B1.ap[:], b1_d, writes=[B1], nbytes=1 << 19)
            fw.dma(SP, G2.ap[:], g2_d, writes=[G2], nbytes=1 << 19); fw.dma(SP, B2.ap[:], b2_d, writes=[B2], nbytes=1 << 19)
        fw.dma(SP, esink.ap[:], sinks_d, writes=[esink])
        fw.dma(SP, pscale.ap[:], pscale_d, writes=[pscale])
        fw.op(ACT, lambda h: h.activation(esink.ap[:], esink.ap[:], AF.Exp), reads=[esink], writes=[esink])
        fw.op(DVE, lambda h: h.memset(cm.ap[:], -0.5), writes=[cm])
        fw.op(DVE, lambda h: h.memset(ones1.ap[:], 1.0), writes=[ones1])

        xb = [fw.sbuf("xb%d" % i, [128, D], BF16) for i in range(2)]
        xf = [fw.sbuf("xf%d" % i, [128, D], F32) for i in range(2)]
        xT = fw.sbuf("xT", [128, 8, 512], BF16)
        xTh = fw.sbuf("xTh", [128, 8, 128], BF16)
        Wkd = fw.sbuf("Wkd", [128, 8, 256], BF16)
        kT = [fw.sbuf("kT%d" % i, [128, 2, 512], BF16) for i in range(2)]
        Vv = [fw.sbuf("V%d" % i, [128, 4, 130], BF16) for i in range(2)]
        uu = [fw.sbuf("u%d" % i, [128, 4, 512], BF16) for i in range(2)]
        lnb = {}
        for nm in ("A0", "A1", "C0", "C1", "C2", "C3"):
            lnb[nm] = (fw.sbuf("st" + nm, [128, 12], F32), fw.sbuf("mv" + nm, [128, 2], F32),
                       fw.sbuf("ve" + nm, [128, 1], F32), fw.sbuf("rstd" + nm, [128, 1], F32))
        dens = [fw.sbuf("den%d" % j, [128, 8], F32) for j in range(2)]
        rdens = [fw.sbuf("rden%d" % j, [128, 8], F32) for j in range(2)]
        y1 = fw.sbuf("y1", [128, D], F32)
        h1 = [fw.sbuf("h1_%d" % i, [128, D], F32) for i in range(4)]
        hbA = fw.sbuf("hbA", [128, D], BF16)
        hbC = [fw.sbuf("hbC%d" % i, [128, D], BF16) for i in range(2)]
        rl = [fw.sbuf("rl%d" % i, [128, 512], F32) for i in range(2)]
        h2 = [fw.sbuf("h2%d" % i, [128, D], F32) for i in range(2)]
        h2T = [fw.sbuf("h2T%d" % i, [128, 8, 128], BF16) for i in range(2)]
        pb = fw.sbuf("pb", [128, 4, 256], BF16)
        pT = [fw.sbuf("pT%d" % i, [128, 2, 128], BF16) for i in range(2)]
        tt = [[fw.sbuf("tt%d_%d" % (j, i), [128, 512], F32) for i in range(2)] for j in range(2)]
        ovl = st.enter_context(nc.sbuf_tensor("ovl", [128, 32 * 512], BF16))
        aT = Buf("aT", ovl)
        o = 0
        def carve(name, n):
            nonlocal o
            b = Buf(name, ovl[:, o:o + n]); o += n
            return b
        yTo = o
        yTb = [carve("yT%d" % i, 8 * 128) for i in range(4)]
        qH = carve("qH", 4 * 512); qT = carve("qT", 4 * 512)
        eo = o
        EEs = [[carve("E%d_%d" % (j, i), 512) for i in range(4)] for j in range(2)]
        dT = Buf("dT", ovl[:, eo:eo + 2048])
        yatts = [carve("yatt%d" % j, 512) for j in range(2)]
        y1b = carve("y1b", 2048)
        hbAb = carve("hbAb", 1024)
        assert o <= 32 * 512, o
        stageA = yTb + [dT, qH, qT, y1b, hbAb] + yatts + EEs[0] + EEs[1]
        aT3 = ovl[:, :].rearrange("p (c t) -> p c t", t=512)
        yT4 = ovl[:, yTo:yTo + 4096].rearrange("p (b c t) -> p b c t", b=4, c=8)
        dT3 = dT.ap.rearrange("p (c t) -> p c t", t=512)
        qT3 = qT.ap.rearrange("p (c t) -> p c t", t=512)
        qH3 = qH.ap.rearrange("p (c t) -> p c t", t=512)

        def handoff(src, dst):
            for d in dst:
                for s in src:
                    if s.last_write is not None:
                        d.reads.append(s.last_write)
                    d.reads.extend(s.reads)

        for v in Vv:
            fw.op(DVE, lambda h, v=v: h.memset(v.ap[:], 1.0), writes=[v])

        fw.dma(SP, y1.ap[0:1, :], bg_d, writes=[y1])
        fw.op(DVE, lambda h: h.tensor_copy(bghl.ap[0:1, 0:D], y1.ap[0:1, :]), reads=[y1], writes=[bghl])
        fw.op(DVE, lambda h: h.tensor_copy(h2[0].ap[0:1, :], bghl.ap[0:1, 0:D]), reads=[bghl], writes=[h2[0]])
        fw.op(DVE, lambda h: h.tensor_tensor(h2[0].ap[0:1, :], y1.ap[0:1, :], h2[0].ap[0:1, :], op=ALU.subtract),
              reads=[y1, h2[0]], writes=[h2[0]])
        fw.op(DVE, lambda h: h.tensor_copy(bghl.ap[0:1, D:2 * D], h2[0].ap[0:1, :]), reads=[h2[0]], writes=[bghl])

        banks = [fw.psum("bk%d" % i, [128, 512], F32) for i in range(8)]
        bstate = {"i": 0, "held": set()}

        def nbg(n=1):
            while True:
                free = [(bstate["i"] + j) % 8 for j in range(8) if ((bstate["i"] + j) % 8) not in bstate["held"]]
                if len(free) >= n:
                    break
                yield
            got = free[:n]
            bstate["i"] = (got[-1] + 1) % 8
            for g_ in got:
                bstate["held"].add(g_)
            return [banks[g_] for g_ in got]

        def fb(bk):
            return lambda: bstate["held"].discard(banks.index(bk))

        slots = [fw.sbuf("ws%d" % i, [128, 4096], BF16) for i in range(NSLOT)]
        pieces = []
        for t in range(4):
            pieces += [(win3[:, :, 512:1024], 8, 512), (win3[:, :, 1024:1152], 8, 128),
                       (win3[:, :, 0:512], 8, 512), (win3[:, :, 1152:1280], 8, 128),
                       (wo3[:, :, 0:512], 8, 512), (wo3[:, :, 512:1024], 8, 512)]
            pieces += [(w13[:, :, fg * 512:(fg + 1) * 512], 8, 512) for fg in range(8)]
            pieces += [(w23[:, cg * 8:(cg + 1) * 8, hf * 512:(hf + 1) * 512], 8, 512)
                       for hf in range(2) for cg in range(4)]
            pieces += [(wg3[:, :, 0:512], 8, 512), (wg3[:, :, 512:1024], 8, 512), (wp3, 2, 1024)]
        ring = {"load": 0}
        free_slots = list(range(NSLOT))
        piece_slot = {}
        piece_ok = set()

        def pview(i, sl):
            src, k, c = pieces[i]
            return slots[sl].ap[:, 0:k * c].rearrange("p (k c) -> p k c", c=c)

        def try_load(maxn=None):
            n = 0
            while free_slots and ring["load"] < len(pieces) and (maxn is None or n < maxn):
                i = ring["load"]; ring["load"] += 1
                sl = free_slots.pop(0)
                piece_slot[i] = sl
                k, c = pieces[i][1], pieces[i][2]
                fw.dma(POOL, pview(i, sl), pieces[i][0], writes=[slots[sl]], nbytes=128 * k * c * 4,
                       on_commit=lambda i=i: piece_ok.add(i))
                n += 1

        def acquire(i):
            while i not in piece_ok:
                yield
            sl = piece_slot[i]
            return slots[sl], pview(i, sl)

        def release(i):
            free_slots.append(piece_slot[i])
            try_load()

        PT = 25
        cnt = {"ev": 0}

        def evac(dst_ap, src_ap, bk, writes, eng=None, c=0.65):
            if eng is None:
                eng = DVE if (cnt["ev"] % 4 == 3) else ACT
                cnt["ev"] += 1
            if eng is ACT:
                fw.op(ACT, lambda h: h.activation(dst_ap, src_ap, AF.Copy), reads=[bk], writes=writes, c=c, done=fb(bk))
            else:
                fw.op(DVE, lambda h: h.tensor_copy(dst_ap, src_ap), reads=[bk], writes=writes, c=c, done=fb(bk))

        def transpose_to(src_buf, src_ap_fn, n, dst_buf, dst_ap, eng=None):
            (bk,) = yield from nbg()
            pv = bk.ap[:].bitcast(BF16)
            for k in range(n):
                fw.op(PE, lambda h, k=k: h.transpose(pv[:, k * 128:(k + 1) * 128], src_ap_fn(k), ident.ap[:]),
                      reads=[src_buf, ident], writes=[bk], signal=(k == n - 1), c=0.07)
            yield
            evac(dst_ap, pv[:, 0:n * 128].rearrange("p (k t) -> p k t", t=128), bk, [dst_buf], eng, c=0.2 + n * 0.07)
            yield

        def load_xb(gbl):
            s = xb[(gbl + 1) % 2]
            fw.dma(POOL, s.ap[:], x[(gbl + 1) * 128:(gbl + 2) * 128, :], writes=[s], nbytes=1 << 19)
            return s

        def load_xf(gbl):
            s = xf[gbl % 2]
            fw.dma(SP, s.ap[:], x[(gbl + 1) * 128:(gbl + 2) * 128, :], writes=[s], nbytes=1 << 19)
            return s

        def uv_block(xT_ap_fn, xT_buf, wu_b, wu, wv_b, wv, par, blk):
            bu, bv = yield from nbg(2)
            for k in range(8):
                fw.op(PE, lambda h, k=k: h.matmul(bu.ap[:], xT_ap_fn(k), wu[:, k, :], start=(k == 0), stop=(k == 7)),
                      reads=[xT_buf, wu_b], writes=[bu], signal=(k == 7))
            for k in range(8):
                fw.op(PE, lambda h, k=k: h.matmul(bv.ap[:, 0:128], xT_ap_fn(k), wv[:, k, :], start=(k == 0), stop=(k == 7)),
                      reads=[xT_buf, wv_b], writes=[bv], signal=(k == 7), c=0.07)
            yield
            evac(uu[par].ap[:, blk, :], bu.ap[:], bu, [uu[par]])
            evac(Vv[par].ap[:, blk, :].rearrange("p (h d) -> p h d", d=65)[:, :, 0:64],
                 bv.ap[:, 0:128].rearrange("p (h d) -> p h d", d=64), bv, [Vv[par]], c=0.3)
            yield

        def layernorm(nm, src, work, G, B, dst, hbuf):
            stb, mvb, veb, rsb = lnb[nm]
            (sb, sap), (wb, wap), (db, dap) = src, work, dst
            for i in range(2):
                fw.op(DVE, lambda h, i=i: h.bn_stats(stb.ap[:, i * 6:(i + 1) * 6], sap[:, i * 512:(i + 1) * 512]),
                      reads=[sb], writes=[stb], c=0.65)
            fw.op(DVE, lambda h: h.bn_aggr(mvb.ap[:], stb.ap[:]), reads=[stb], writes=[mvb], c=0.2)
            yield
            fw.op(POOL, lambda h: h.tensor_scalar(veb.ap[:], mvb.ap[:, 1:2], EPS, None, op0=ALU.add), reads=[mvb], writes=[veb], c=0.2)
            fw.op(POOL, lambda h: h.tensor_tensor(rsb.ap[:], veb.ap[:], cm.ap[:], op=ALU.pow), reads=[veb, cm], writes=[rsb], c=0.5)
            yield
            fw.op(DVE, lambda h: h.tensor_scalar(wap, sap, mvb.ap[:, 0:1], rsb.ap[:, 0:1],
                                                 op0=ALU.subtract, op1=ALU.mult), reads=[sb, mvb, rsb], writes=[wb], c=0.8)
            fw.op(DVE, lambda h: h.tensor_tensor(wap, wap, G.ap[:], op=ALU.mult), reads=[wb, G], writes=[wb], c=1.2)
            yield
            fw.op(DVE, lambda h: h.tensor_tensor(hbuf[1], wap, B.ap[:], op=ALU.add), reads=[wb, B], writes=[hbuf[0]], c=1.2)
            yield
            fw.op(POOL, lambda h: h.tensor_tensor(dap, wap, B.ap[:], op=ALU.add), reads=[wb, B], writes=[db], c=2.4)
            yield

        prog = {"c_read": -1, "attn": -1, "wln": -1, "B": -1}
        wost = {}
        attn_done = set()
        wln_cnt = {}
        st_ = {"xb_next": None}

        def gen_Apre(t):
            par = t % 2
            p0 = t * PT
            if t == 0:
                s = st_["xb_next"]
                st_["xb_next"] = load_xb(0)
                try_load(2)
                yield from transpose_to(s, lambda k, s=s: s.ap[:, k * 128:(k + 1) * 128], 8, xTh, xTh.ap[:])
            for b in range(4):
                gbl = t * 4 + b
                s = st_["xb_next"]
                if gbl + 1 < NBLK:
                    st_["xb_next"] = load_xb(gbl + 1)
                if t == 0 and b == 2:
                    try_load()
                    late_consts()
                    late_consts_sp()
                yield from transpose_to(s, lambda k, s=s: s.ap[:, k * 128:(k + 1) * 128], 8, xT, xT.ap[:, :, b * 128:(b + 1) * 128])
            handoff(EEs[0], [dT])
            fw.op(POOL, lambda h: h.memset(qT3[64:128, :, :], 0.0), writes=[qT], c=1.0)
            fw.op(POOL, lambda h: h.memset(qH3[0:64, :, :], 0.0), writes=[qH], c=1.0)
            wq_b, wq = yield from acquire(p0 + 0)
            for qc in range(4):
                (bk,) = yield from nbg()
                for k in range(8):
                    fw.op(PE, lambda h, k=k, qc=qc, bk=bk: h.matmul(bk.ap[:], wq[:, k, qc * 128:(qc + 1) * 128], xT.ap[:, k, :],
                                                                    start=(k == 0), stop=(k == 7)),
                          reads=[wq_b, xT], writes=[bk], signal=(k == 7))
                yield
                fw.op(ACT, lambda h, qc=qc, bk=bk: h.activation(qT3[0:64, qc, :], bk.ap[0:64, :], AF.Copy),
                      reads=[bk], writes=[qT], c=0.65)
                fw.op(ACT, lambda h, qc=qc, bk=bk: h.activation(qH3[64:128, qc, :], bk.ap[64:128, :], AF.Copy),
                      reads=[bk], writes=[qH], c=0.65, done=fb(bk))
                yield
            release(p0 + 0)
            wk_b, wk = yield from acquire(p0 + 1)
            for kvh in range(2):
                for r in range(2):
                    fw.op(DVE, lambda h, kvh=kvh, r=r: h.tensor_copy(Wkd.ap[:, :, kvh * 128 + r * 64:kvh * 128 + r * 64 + 64],
                                                                    wk[:, :, kvh * 64:(kvh + 1) * 64]),
                          reads=[wk_b], writes=[Wkd], c=0.4)
            yield
            release(p0 + 1)
            for kvh in range(2):
                (bk,) = yield from nbg()
                for k in range(8):
                    fw.op(PE, lambda h, k=k, kvh=kvh, bk=bk: h.matmul(bk.ap[:], Wkd.ap[:, k, kvh * 128:(kvh + 1) * 128], xT.ap[:, k, :],
                                                                      start=(k == 0), stop=(k == 7)),
                          reads=[Wkd, xT], writes=[bk], signal=(k == 7))
                yield
                evac(kT[par].ap[:, kvh, :], bk.ap[:], bk, [kT[par]])
                yield
            if t == 0:
                (bk,) = yield from nbg()
                for kvh in range(2):
                    for k in range(8):
                        fw.op(PE, lambda h, k=k, kvh=kvh, bk=bk: h.matmul(bk.ap[:, kvh * 128:(kvh + 1) * 128],
                                                                          Wkd.ap[:, k, kvh * 128:(kvh + 1) * 128], xTh.ap[:, k, :],
                                                                          start=(k == 0), stop=(k == 7)),
                              reads=[Wkd, xTh], writes=[bk], signal=(k == 7), c=0.07)
                yield
                evac(kT[1].ap[:, :, 384:512], bk.ap[:, 0:256].rearrange("p (h t) -> p h t", t=128), bk, [kT[1]], c=0.4)
                yield
            wu_b, wu = yield from acquire(p0 + 2)
            wv_b, wv = yield from acquire(p0 + 3)
            if t == 0:
                yield from uv_block(lambda k: xTh.ap[:, k, :], xTh, wu_b, wu, wv_b, wv, 1, 3)
            for b in range(4):
                yield from uv_block(lambda k, b=b: xT.ap[:, k, b * 128:(b + 1) * 128], xT, wu_b, wu, wv_b, wv, par, b)
            release(p0 + 2); release(p0 + 3)
            for b in range(4):
                first = (t == 0 and b == 0)
                if b == 0:
                    up_b, up = uu[1 - par], uu[1 - par].ap[:, 3, :]
                else:
                    up_b, up = uu[par], uu[par].ap[:, b - 1, :]
                uc = uu[par].ap[:, b, :]
                (bk,) = yield from nbg()
                for g in range(4):
                    acur_b, acur = (Apf, Apf.ap[:, g, :]) if first else (Ap, Ap.ap[:, 4 + g, :])
                    fw.op(PE, lambda h, g=g, bk=bk, up=up: h.matmul(bk.ap[:, g * 128:(g + 1) * 128], up[:, g * 128:(g + 1) * 128],
                                                                    Ap.ap[:, g, :], start=True, stop=False),
                          reads=[up_b, Ap], writes=[bk], signal=False, c=0.07)
                    fw.op(PE, lambda h, g=g, bk=bk, uc=uc, acur=acur: h.matmul(bk.ap[:, g * 128:(g + 1) * 128], uc[:, g * 128:(g + 1) * 128],
                                                                                 acur, start=False, stop=True),
                          reads=[uu[par], acur_b], writes=[bk], signal=(g == 3), c=0.07)
                yield
                evac(dT3[:, :, b * 128:(b + 1) * 128], bk.ap[:].rearrange("p (g t) -> p g t", t=128), bk, [dT])
                yield
            for g in range(4):
                (bk,) = yield from nbg()
                fw.op(PE, lambda h, g=g, bk=bk: h.matmul(bk.ap[:], Wp.ap[:, g, :], dT3[:, g, :], start=True, stop=True),
                      reads=[Wp, dT], writes=[bk])
                yield
                fw.op(ACT, lambda h, g=g, bk=bk: h.activation(yT4[:, :, g, :], bk.ap[:].rearrange("p (b t) -> p b t", t=128),
                                                              AF.Copy, scale=pscale.ap[:, g:g + 1]),
                      reads=[bk, pscale], writes=yTb, done=fb(bk))
                yield
            handoff([dT], EEs[0])
            wo0_b, wo0 = yield from acquire(p0 + 4)
            wo1_b, wo1 = yield from acquire(p0 + 5)
            wost[t] = [(wo0_b, wo0), (wo1_b, wo1)]

        def gen_attn(t, s2):
            par = t % 2
            EE = EEs[s2]; yatt = yatts[s2]; den = dens[s2]; rden = rdens[s2]
            for b in (s2, s2 + 2):
                gbl = t * 4 + b
                first = (gbl == 0)
                if b == 0:
                    kp_b, kp = kT[1 - par], kT[1 - par].ap[:, :, 384:512]
                    vp_b, vp = Vv[1 - par], Vv[1 - par].ap[:, 3, :]
                else:
                    kp_b, kp = kT[par], kT[par].ap[:, :, (b - 1) * 128:b * 128]
                    vp_b, vp = Vv[par], Vv[par].ap[:, b - 1, :]
                kc = kT[par].ap[:, :, b * 128:(b + 1) * 128]
                vc = Vv[par].ap[:, b, :]
                for kvh in range(2):
                    for half in range(2):
                        (bk,) = yield from nbg()
                        for kb in range(2):
                            kk_b, kk = (kp_b, kp) if kb == 0 else (kT[par], kc)
                            for gp in range(2):
                                qc = 2 * kvh + gp
                                col = (kb * 2 + gp) * 128
                                qz_b, qz3 = (qT, qT3) if half == 0 else (qH, qH3)
                                fw.op(PE, lambda h, bk=bk, col=col, kk=kk, kvh=kvh, qc=qc, qz3=qz3, b=b:
                                      h.matmul(bk.ap[:, col:col + 128], kk[:, kvh, :],
                                               qz3[:, qc, b * 128:(b + 1) * 128], start=True, stop=True),
                                      reads=[kk_b, qz_b], writes=[bk], signal=(kb == 1 and gp == 1), c=0.07)
                        yield
                        E = EE[kvh * 2 + half]
                        fw.op(ACT, lambda h, bk=bk, E=E: h.activation(E.ap, bk.ap[:], AF.Exp, scale=0.125), reads=[bk], writes=[E],
                              c=0.5, done=fb(bk))
                        yield
                        m_b = Mf if first else Mk
                        m = m_b.ap[:, kvh * 2 + half, :]
                        fw.op(DVE, lambda h, E=E, m=m: h.tensor_tensor(E.ap, E.ap, m, op=ALU.mult), reads=[E, m_b], writes=[E], c=0.43)
                        yield
                bos = yield from nbg(2)
                for kvh in range(2):
                    bo = bos[kvh]
                    for g in range(4):
                        gp, half = g // 2, g % 2
                        E = EE[kvh * 2 + half]
                        for kb in range(2):
                            col = (kb * 2 + gp) * 128
                            vv_b, vv = (vp_b, vp) if kb == 0 else (Vv[par], vc)
                            fw.op(PE, lambda h, bo=bo, g=g, E=E, vv=vv, kvh=kvh, kb=kb, col=col:
                                  h.matmul(bo.ap[:, g * 65:(g + 1) * 65], E.ap[:, col:col + 128],
                                           vv[:, kvh * 65:(kvh + 1) * 65], start=(kb == 0), stop=(kb == 1)),
                                  reads=[E, vv_b], writes=[bo], signal=(g == 3 and kb == 1), c=0.07)
                    yield
                for kvh in range(2):
                    bo = bos[kvh]
                    bo3 = bo.ap[:, 0:260].rearrange("p (h d) -> p h d", d=65)
                    fw.op(DVE, lambda h, bo3=bo3, kvh=kvh: h.tensor_tensor(den.ap[:, kvh * 4:(kvh + 1) * 4], bo3[:, :, 64],
                                                                           esink.ap[:, kvh * 4:(kvh + 1) * 4], op=ALU.add),
                          reads=[bo, esink], writes=[den], c=0.15)
                    fw.op(DVE, lambda h, kvh=kvh: h.reciprocal(rden.ap[:, kvh * 4:(kvh + 1) * 4], den.ap[:, kvh * 4:(kvh + 1) * 4]),
                          reads=[den], writes=[rden], c=0.2)
                    yield
                    for g in range(4):
                        hh = 4 * kvh + g
                        dn = fb(bo) if g == 3 else None
                        if False:
                            fw.op(DVE, lambda h, bo3=bo3, g=g, hh=hh: h.tensor_scalar(yatt.ap[:, hh * 64:(hh + 1) * 64], bo3[:, g, 0:64],
                                                                                      rden.ap[:, hh:hh + 1], None, op0=ALU.mult),
                                  reads=[bo, rden], writes=[yatt], c=0.22, done=dn)
                        else:
                            fw.op(ACT, lambda h, bo3=bo3, g=g, hh=hh: h.activation(yatt.ap[:, hh * 64:(hh + 1) * 64], bo3[:, g, 0:64],
                                                                                   AF.Copy, scale=rden.ap[:, hh:hh + 1]),
                                  reads=[bo, rden], writes=[yatt], c=0.32, done=dn)
                    yield
                yield from transpose_to(yatt, lambda k: yatt.ap[:, k * 128:(k + 1) * 128], 4, yTb[b], yT4[:, b, 4:8, :])
                attn_done.add(gbl)

        def gen_wln(t, s2):
            wo = wost[t]
            p0 = t * PT
            if s2 == 0:
                y1_b, y1_ap, hb_b, hb_ap = y1, y1.ap[:], hbA, hbA.ap[:]
            else:
                y1_b, y1_ap, hb_b, hb_ap = y1b, y1b.ap.bitcast(F32), hbAb, hbAb.ap
            xfs = load_xf(t * 4 + s2)
            for b in (s2, s2 + 2):
                gbl = t * 4 + b
                while gbl not in attn_done:
                    yield
                for hf in range(2):
                    (bk,) = yield from nbg()
                    wb_, wv_ = wo[hf]
                    for c in range(8):
                        fw.op(PE, lambda h, bk=bk, c=c, wv_=wv_, b=b: h.matmul(bk.ap[:], yT4[:, b, c, :], wv_[:, c, :],
                                                                              start=(c == 0), stop=(c == 7)),
                              reads=[yTb[b], wb_], writes=[bk], signal=(c == 7))
                    yield
                    fw.op(DVE, lambda h, bk=bk, hf=hf, xfs=xfs: h.scalar_tensor_tensor(y1_ap[:, hf * 512:(hf + 1) * 512],
                                                                                       xfs.ap[:, hf * 512:(hf + 1) * 512], ALPHA, bk.ap[:],
                                                                                       op0=ALU.mult, op1=ALU.add),
                          reads=[xfs, bk], writes=[y1_b], c=0.7, done=fb(bk))
                    yield
                if b == s2:
                    xfs = load_xf(gbl + 2)
                while gbl >= 4 and (gbl - 4) not in cst["read"]:
                    yield
                yield from layernorm("A%d" % s2, (y1_b, y1_ap), (y1_b, y1_ap), G1, B1, (h1[b], h1[b].ap[:]), (hb_b, hb_ap))
                yield from transpose_to(hb_b, lambda k: hb_ap[:, k * 128:(k + 1) * 128], 8, xT, xT.ap[:, :, b * 128:(b + 1) * 128])
            wln_cnt[t] = wln_cnt.get(t, 0) + 1
            if wln_cnt[t] == 2:
                release(p0 + 4); release(p0 + 5)
                yield
                prog["wln"] = t

        def gen_B(t):
            p0 = t * PT
            handoff(stageA, [aT])
            for fg in range(8):
                w1_b, w1 = yield from acquire(p0 + 6 + fg)
                for fc in range(4):
                    c = fg * 4 + fc
                    (bk,) = yield from nbg()
                    for k in range(8):
                        fw.op(PE, lambda h, bk=bk, k=k, fc=fc, w1=w1: h.matmul(bk.ap[:], w1[:, k, fc * 128:(fc + 1) * 128], xT.ap[:, k, :],
                                                                              start=(k == 0), stop=(k == 7)),
                              reads=[w1_b, xT], writes=[bk], signal=(k == 7))
                    yield
                    r = rl[c % 2]
                    fw.op(ACT, lambda h, bk=bk, r=r: h.activation(r.ap[:], bk.ap[:], AF.Relu), reads=[bk], writes=[r], c=0.6, done=fb(bk))
                    yield
                    fw.op(DVE, lambda h, r=r, c=c: h.tensor_tensor(aT3[:, c, :], r.ap[:], r.ap[:], op=ALU.mult), reads=[r], writes=[aT], c=0.62)
                    yield
                release(p0 + 6 + fg)
            for hf in range(2):
                acc = yield from nbg(4)
                for cg in range(4):
                    w2_b, w2 = yield from acquire(p0 + 14 + hf * 4 + cg)
                    for j in range(8):
                        c = cg * 8 + j
                        for b in range(4):
                            fw.op(PE, lambda h, c=c, j=j, b=b, w2=w2, acc=acc: h.matmul(acc[b].ap[:], aT3[:, c, b * 128:(b + 1) * 128], w2[:, j, :],
                                                                                        start=(c == 0), stop=(c == 31)),
                                  reads=[aT, w2_b], writes=[acc[b]], signal=(c == 31 or (j == 7 and b == 3)))
                        if j % 2 == 1:
                            yield
                    release(p0 + 14 + hf * 4 + cg)
                for b in range(4):
                    fw.op(DVE, lambda h, b=b, hf=hf, acc=acc: h.scalar_tensor_tensor(h1[b].ap[:, hf * 512:(hf + 1) * 512],
                                                                                     h1[b].ap[:, hf * 512:(hf + 1) * 512], ALPHA, acc[b].ap[:],
                                                                                     op0=ALU.mult, op1=ALU.add),
                          reads=[h1[b], acc[b]], writes=[h1[b]], c=0.7, done=fb(acc[b]))
                    yield
            handoff([aT], stageA)
            prog["B"] = t

        cst = {"done": [-1, -1, -1, -1], "read": set(), "pb": -1, "rel": {}}
        cbufs = {cp_: dict(hs=h2[cp_], hbc=hbC[cp_], hT=h2T[cp_], pT=pT[cp_], tt=tt[cp_]) for cp_ in range(2)}
        oo = 0
        for cp_ in (2, 3):
            def cv(name, n, f32=False, r3=None):
                nonlocal oo
                apv = ovl[:, oo:oo + n]
                oo += n
                if f32:
                    apv = apv.bitcast(F32)
                if r3:
                    apv = apv.rearrange("p (k t) -> p k t", t=r3)
                return Buf(name + str(cp_), apv)
            cbufs[cp_] = dict(hs=cv("xh2_", 2048, f32=True), hbc=cv("xhbC_", 1024), hT=cv("xh2T_", 1024, r3=128),
                              pT=cv("xpT_", 256, r3=128), tt=[cv("xtt0_", 1024, f32=True), cv("xtt1_", 1024, f32=True)])

        def gen_C(t, cp, blocks, nstreams):
            p0 = t * PT
            cb = cbufs[cp]
            wg0_b, wg0 = yield from acquire(p0 + 22)
            wg1_b, wg1 = yield from acquire(p0 + 23)
            wpl_b, wpl = yield from acquire(p0 + 24)
            wgs = [(wg0_b, wg0), (wg1_b, wg1)]
            if cp == 0:
                while cst["done"][1] < t - 1:
                    yield
                fw.dma(POOL, pb.ap[:], p[t * 512:(t + 1) * 512, :].rearrange("(b q) c -> q b c", q=128), writes=[pb], nbytes=1 << 19)
                yield
                cst["pb"] = t
            else:
                while cst["pb"] < t:
                    yield
            for b in blocks:
                gbl = t * 4 + b
                hs = cb["hs"]
                hbc = cb["hbc"]
                yield from layernorm("C%d" % cp, (h1[b], h1[b].ap[:]), (hs, hs.ap[:]), G2, B2, (hs, hs.ap[:]), (hbc, hbc.ap[:]))
                cst["read"].add(gbl)
                hT = cb["hT"]
                yield from transpose_to(hbc, lambda k, hbc=hbc: hbc.ap[:, k * 128:(k + 1) * 128], 8, hT, hT.ap[:])
                pTs = cb["pT"]
                yield from transpose_to(pb, lambda k, b=b: pb.ap[:, b, k * 128:(k + 1) * 128], 2, pTs, pTs.ap[:])
                for hf in range(2):
                    bg_, bp_ = yield from nbg(2)
                    wb_, wv_ = wgs[hf]
                    for k in range(8):
                        fw.op(PE, lambda h, bg_=bg_, k=k, wv_=wv_, hT=hT: h.matmul(bg_.ap[:], hT.ap[:, k, :], wv_[:, k, :], start=(k == 0), stop=False),
                              reads=[hT, wb_], writes=[bg_], signal=False)
                    for r in range(2):
                        fw.op(PE, lambda h, bg_=bg_, r=r, hf=hf: h.matmul(bg_.ap[:], ones1.ap[0:1, :],
                                                                          bghl.ap[0:1, r * D + hf * 512:r * D + (hf + 1) * 512],
                                                                          start=False, stop=(r == 1)),
                              reads=[ones1, bghl], writes=[bg_], signal=(r == 1))
                    for k in range(2):
                        fw.op(PE, lambda h, bp_=bp_, k=k, hf=hf, pTs=pTs: h.matmul(bp_.ap[:], pTs.ap[:, k, :], wpl[:, k, hf * 512:(hf + 1) * 512],
                                                                                   start=(k == 0), stop=(k == 1)),
                              reads=[pTs, wpl_b], writes=[bp_], signal=(k == 1))
                    yield
                    tts = cb["tt"][hf]
                    fw.op(ACT, lambda h, bg_=bg_, tts=tts: h.activation(tts.ap[:], bg_.ap[:], AF.Tanh, scale=0.5), reads=[bg_], writes=[tts],
                          c=0.6, done=fb(bg_))
                    yield
                    fw.op(DVE, lambda h, tts=tts, bp_=bp_: h.scalar_tensor_tensor(tts.ap[:], tts.ap[:], 1.0, bp_.ap[:], op0=ALU.add, op1=ALU.mult),
                          reads=[tts, bp_], writes=[tts], c=0.7, done=fb(bp_))
                    fw.op(DVE, lambda h, tts=tts, hs=hs, hf=hf: h.scalar_tensor_tensor(hs.ap[:, hf * 512:(hf + 1) * 512], tts.ap[:], 0.5,
                                                                                       hs.ap[:, hf * 512:(hf + 1) * 512], op0=ALU.mult, op1=ALU.add),
                          reads=[tts, hs], writes=[hs], c=0.7)
                    yield
                fw.dma(SP, out[gbl * 128:(gbl + 1) * 128, :], hs.ap[:], reads=[hs], sem_buf=hs, nbytes=1 << 19)
                yield
            cst["done"][cp] = t
            cst["rel"][t] = cst["rel"].get(t, 0) + 1
            if cst["rel"][t] == nstreams:
                release(p0 + 22); release(p0 + 23); release(p0 + 24)
                yield

        def main_line():
            for t in range(4):
                yield from gen_Apre(t)
                fw.spawn(gen_attn(t, 0))
                fw.spawn(gen_attn(t, 1))
                fw.spawn(gen_wln(t, 0))
                fw.spawn(gen_wln(t, 1))
                while prog["wln"] < t:
                    yield
                yield from gen_B(t)

        def c_line(cp):
            for t in range(4):
                while prog["B"] < t:
                    yield
                if t < 3:
                    yield from gen_C(t, cp, (cp, cp + 2), 2)
                else:
                    yield from gen_C(t, cp, (cp,), 4)

        def c_line_x(cp):
            while prog["B"] < 3:
                yield
            handoff([aT], [v for v in cbufs[cp].values() if isinstance(v, Buf)] + cbufs[cp]["tt"])
            yield from gen_C(3, cp, (cp,), 4)

        st_["xb_next"] = load_xb(-1)
        fw.schedule([main_line(), c_line(0), c_line(1), c_line_x(2), c_line_x(3)])
        fw.final_wait(SP, h2 + [cbufs[2]["hs"], cbufs[3]["hs"]])
        fw.emit()
        build_nc.stats = {e.name: (e.n_inst, e.n_wait, round(e.free, 1)) for e in fw.engs.values()}
    return nc


def _constants():
    j = np.arange(128)[:, None]
    q = np.arange(128)[None, :]
    mask = np.zeros((128, 2, 2, 2, 2, 128), np.float32)
    for kvh in range(2):
        for half in range(2):
            for gp in range(2):
                slope = 2.0 ** (-(4 * kvh + 2 * gp + half + 1))
                dprev = (q + 128 - j).astype(np.float32)
                dcur = (q - j).astype(np.float32)
                mask[:, kvh, half, 0, gp, :] = np.where(j > q, np.exp(-slope * dprev), 0.0)
                mask[:, kvh, half, 1, gp, :] = np.where(j <= q, np.exp(-slope * dcur), 0.0)
    apool = np.zeros((128, 2, 4, 128), np.float32)
    apool_first = np.zeros((128, 4, 128), np.float32)
    si = np.arange(128)[:, None]
    ti = np.arange(128)[None, :]
    for g, w in enumerate((2, 4, 8, 16)):
        apool[:, 0, g, :] = np.where(128 + ti - si < w, 1.0 / w, 0.0)
        cur = np.where((si <= ti) & (ti - si < w), 1.0 / w, 0.0)
        apool[:, 1, g, :] = cur - (si == ti)
        cnt = np.minimum(ti + 1, w).astype(np.float32)
        apool_first[:, g, :] = np.where((si <= ti) & (ti - si < w), 1.0 / cnt, 0.0) - (si == ti)
    return mask, apool, apool_first


def _make_in_maps(x, p, w_in, w_pool, pool_scale, attn_sinks, w_out, ln1_g, ln1_b,
                  w_ff1, w_ff2, ln2_g, ln2_b, w_ple, w_ple_gate, b_ple_gate):
    f = lambda a: np.ascontiguousarray(np.asarray(a, dtype=np.float32))
    x = f(x); p = f(p)
    mask, apool, apool_first = _constants()
    rep = lambda v: np.ascontiguousarray(np.broadcast_to(f(v).reshape(1, -1), (128, f(v).size)))
    shared = {
        "w_in": f(w_in)[0], "w_pool": f(w_pool)[0],
        "pool_scale": np.ascontiguousarray(f(pool_scale)[0].reshape(4, 128).T),
        "attn_sinks": rep(attn_sinks[0]), "w_out": f(w_out)[0],
        "ln1_g": rep(ln1_g[0]), "ln1_b": rep(ln1_b[0]), "w_ff1": f(w_ff1)[0], "w_ff2": f(w_ff2)[0],
        "ln2_g": rep(ln2_g[0]), "ln2_b": rep(ln2_b[0]), "w_ple": f(w_ple)[0], "w_ple_gate": f(w_ple_gate)[0],
        "b_ple_gate": f(b_ple_gate)[0].reshape(1, D),
        "c_ident": np.eye(128, dtype=np.float32),
        "c_mask": np.ascontiguousarray(mask.reshape(128, 4, 512)),
        "c_apool": np.ascontiguousarray(apool.reshape(128, 8, 128)),
    }
    in_maps = []
    for c in range(8):
        bi, j = c // 4, c % 4
        xs = np.zeros((TOK + 128, D), np.float32)
        lo = j * TOK
        xs[128:] = x[bi, lo:lo + TOK]
        if j > 0:
            xs[:128] = x[bi, lo - 128:lo]
        m = dict(shared)
        m["x"] = xs
        m["p"] = np.ascontiguousarray(p[0, bi, lo:lo + TOK])
        mf = mask.copy()
        if j == 0:
            mf[:, :, :, 0] = 0.0
            m["c_apool_first"] = np.ascontiguousarray(apool_first)
        else:
            m["c_apool_first"] = np.ascontiguousarray(apool[:, 1])
        m["c_mask_first"] = np.ascontiguousarray(mf.reshape(128, 4, 512))
        in_maps.append(m)
    return in_maps


def kernel(**inputs):
    in_maps = _make_in_maps(**inputs)
    nc = build_nc()
    res = run_bass_kernel_spmd(nc, in_maps, core_ids=list(range(8)))
    outp = np.zeros((2, 8192, D), np.float32)
    for c in range(8):
        bi, j = c // 4, c % 4
        outp[bi, j * TOK:(j + 1) * TOK] = res.results[c]["out"]
    return outp
```
